# Optimizing a Trainium2 kernel written in Bass

```python
import jax, jax.numpy as jnp
from jax import lax
import numpy as np

D_MODEL = 1024
BATCH = 8
SEQ = 4096
DEPTH = 2

GRID_W = 64
WIN_H_MAX = 8
WIN_W = 16
COL_BLOCK = 16
COL_SLAB = 32
NA_HEADS = 8
NA_HEAD_DIM = 64
NA_WIDTH = NA_HEADS * NA_HEAD_DIM
GLA_HEADS = 4
GLA_DK = 64
GLA_DV = 128
GLA_QK_WIDTH = GLA_HEADS * GLA_DK
GLA_V_WIDTH = GLA_HEADS * GLA_DV
GLA_GATE_RANK = 16
GLA_TAU = 16.0
GLA_CHUNK = 64
MIX_WIDTH = NA_WIDTH + GLA_V_WIDTH
IN_SPLITS = (NA_WIDTH, NA_WIDTH, NA_WIDTH,
             GLA_QK_WIDTH, GLA_QK_WIDTH,
             GLA_V_WIDTH, GLA_V_WIDTH,
             GLA_GATE_RANK, GLA_GATE_RANK)
IN_WIDTH = sum(IN_SPLITS)
D_FF = 2816
EPS = 1e-6
NEG_INF = -1e30

kernel_name = "hybrid_na_gla_macaron_encoder"


def rmsnorm(x, g):
    xf = x.astype(jnp.float32)
    y = xf * lax.rsqrt(jnp.mean(xf * xf, axis=-1, keepdims=True) + EPS)
    return (y * g.astype(jnp.float32)).astype(x.dtype)


def swiglu(h, w_gate, w_up, w_down):
    return (jax.nn.silu(h @ w_gate) * (h @ w_up)) @ w_down


def split_heads(a, n):
    b, s, w = a.shape
    return a.reshape(b, s, n, w // n).transpose(0, 2, 1, 3)


def merge_heads(a):
    b, n, s, d = a.shape
    return a.transpose(0, 2, 1, 3).reshape(b, s, n * d)


def neighbourhood_attention(q, k, v, rpb):
    B, H, S, d = q.shape
    rows = S // GRID_W
    wh = min(WIN_H_MAX, rows)
    q = q.reshape(B, H, rows, GRID_W, d)
    k = k.reshape(B, H, rows, GRID_W, d)
    v = v.reshape(B, H, rows, GRID_W, d)
    r = np.arange(rows)
    row_start = np.clip(r - wh // 2, 0, rows - wh)
    key_rows = row_start[:, None] + np.arange(wh)[None, :]
    dr = key_rows - r[:, None] + (WIN_H_MAX - 1)
    scale = NA_HEAD_DIM ** -0.5
    outs = []
    for j in range(GRID_W // COL_BLOCK):
        qc = np.arange(j * COL_BLOCK, (j + 1) * COL_BLOCK)
        col_start = np.clip(qc - WIN_W // 2, 0, GRID_W - WIN_W)
        c0 = int(np.clip(j * COL_BLOCK - WIN_W // 2, 0, GRID_W - COL_SLAB))
        kc = c0 + np.arange(COL_SLAB)
        valid = (kc[None, :] >= col_start[:, None]) & (kc[None, :] < col_start[:, None] + WIN_W)
        dc = np.clip(kc[None, :] - qc[:, None] + (WIN_W - 1), 0, 2 * WIN_W - 2)
        qb = q[:, :, :, j * COL_BLOCK:(j + 1) * COL_BLOCK]
        kb = k[:, :, :, c0:c0 + COL_SLAB][:, :, key_rows]
        vb = v[:, :, :, c0:c0 + COL_SLAB][:, :, key_rows]
        s = jnp.einsum('bhrqd,bhrwkd->bhrqwk', qb, kb).astype(jnp.float32) * scale
        bias = rpb[:, dr[:, None, :, None], dc[None, :, None, :]]
        s = s + bias[None].astype(jnp.float32)
        s = jnp.where(valid[:, None, :], s, NEG_INF)
        p = jax.nn.softmax(s.reshape(B, H, rows, COL_BLOCK, wh * COL_SLAB), axis=-1)
        p = p.reshape(B, H, rows, COL_BLOCK, wh, COL_SLAB).astype(v.dtype)
        outs.append(jnp.einsum('bhrqwk,bhrwkd->bhrqd', p, vb))
    o = jnp.concatenate(outs, axis=3)
    return o.reshape(B, H, S, d)


def gla_chunked(q, k, v, g, include_diag):
    B, H, T, dk = q.shape
    dv = v.shape[-1]
    n = T // GLA_CHUNK

    def to_chunks(a):
        return jnp.moveaxis(a.reshape(B, H, n, GLA_CHUNK, a.shape[-1]), 2, 0)

    t = np.arange(GLA_CHUNK)
    mask = (t[:, None] >= t[None, :]) if include_diag else (t[:, None] > t[None, :])

    def step(state, inp):
        qi, ki, vi, gi = inp
        b = jnp.cumsum(gi, axis=-2)
        inter = jnp.einsum('bhtd,bhde->bhte', qi * jnp.exp(b), state)
        diff = b[:, :, :, None, :] - b[:, :, None, :, :]
        decay = jnp.exp(jnp.where(mask[:, :, None], diff, -jnp.inf))
        a = jnp.einsum('bhtd,bhsd,bhtsd->bhts', qi, ki, decay)
        intra = jnp.einsum('bhts,bhse->bhte', a, vi)
        b_last = b[:, :, -1:, :]
        state = (jnp.exp(b_last[:, :, 0, :])[..., None] * state
                 + jnp.einsum('bhsd,bhse->bhde', ki * jnp.exp(b_last - b), vi))
        return state, inter + intra

    s0 = jnp.zeros((B, H, dk, dv), jnp.float32)
    _, o = lax.scan(step, s0, (to_chunks(q), to_chunks(k), to_chunks(v), to_chunks(g)))
    return jnp.moveaxis(o, 0, 2).reshape(B, H, T, dv)


def hybrid_layer(x, ffn1_norm, ffn1_wg, ffn1_wu, ffn1_wd, mix_norm, w_in, na_rpb, na_gain,
                 w_gate_f, b_gate_f, w_gate_b, b_gate_b, gla_gain, w_out,
                 ffn2_norm, ffn2_wg, ffn2_wu, ffn2_wd):
    x = x + 0.5 * swiglu(rmsnorm(x, ffn1_norm), ffn1_wg, ffn1_wu, ffn1_wd)

    h = rmsnorm(x, mix_norm)
    proj = h @ w_in
    offsets = list(np.cumsum(IN_SPLITS)[:-1])
    na_q, na_k, na_v, g_q, g_k, g_v, g_r, gf_code, gb_code = jnp.split(proj, offsets, axis=-1)

    na_o = neighbourhood_attention(split_heads(na_q, NA_HEADS), split_heads(na_k, NA_HEADS),
                                   split_heads(na_v, NA_HEADS), na_rpb)
    na_o = rmsnorm(merge_heads(na_o), na_gain)

    f32 = jnp.float32
    qg = split_heads(g_q, GLA_HEADS).astype(f32) * (GLA_DK ** -0.5)
    kg = split_heads(g_k, GLA_HEADS).astype(f32)
    vg = split_heads(g_v, GLA_HEADS).astype(f32)
    log_gf = jax.nn.log_sigmoid((gf_code @ w_gate_f + b_gate_f).astype(f32)) / GLA_TAU
    log_gb = jax.nn.log_sigmoid((gb_code @ w_gate_b + b_gate_b).astype(f32)) / GLA_TAU
    log_gf = split_heads(log_gf, GLA_HEADS)
    log_gb = split_heads(log_gb, GLA_HEADS)
    o_fwd = gla_chunked(qg, kg, vg, log_gf, True)
    flip = lambda a: jnp.flip(a, axis=2)
    o_bwd = flip(gla_chunked(flip(qg), flip(kg), flip(vg), flip(log_gb), False))
    gla_o = rmsnorm(o_fwd + o_bwd, gla_gain).astype(x.dtype)
    gla_o = merge_heads(gla_o) * jax.nn.silu(g_r)

    x = x + jnp.concatenate([na_o, gla_o], axis=-1) @ w_out

    x = x + 0.5 * swiglu(rmsnorm(x, ffn2_norm), ffn2_wg, ffn2_wu, ffn2_wd)
    return x


def setup_inputs(seed: int = 0) -> dict:
    key = jax.random.key(seed)
    ks = iter(jax.random.split(key, 32))
    nrm = lambda shape, s: jax.random.normal(next(ks), shape, jnp.float32) * s
    gain = lambda shape: 1.0 + nrm(shape, 0.02)
    L, D = DEPTH, D_MODEL
    return {
        "x": nrm((BATCH, SEQ, D), 1.0),
        "ffn1_norm": gain((L, D)),
        "ffn1_wg": nrm((L, D, D_FF), D ** -0.5),
        "ffn1_wu": nrm((L, D, D_FF), D ** -0.5),
        "ffn1_wd": nrm((L, D_FF, D), D_FF ** -0.5),
        "mix_norm": gain((L, D)),
        "w_in": nrm((L, D, IN_WIDTH), D ** -0.5),
        "na_rpb": nrm((L, NA_HEADS, 2 * WIN_H_MAX - 1, 2 * WIN_W - 1), 0.1),
        "na_gain": gain((L, NA_WIDTH)),
        "w_gate_f": nrm((L, GLA_GATE_RANK, GLA_QK_WIDTH), GLA_GATE_RANK ** -0.5),
        "b_gate_f": nrm((L, GLA_QK_WIDTH), 0.1),
        "w_gate_b": nrm((L, GLA_GATE_RANK, GLA_QK_WIDTH), GLA_GATE_RANK ** -0.5),
        "b_gate_b": nrm((L, GLA_QK_WIDTH), 0.1),
        "gla_gain": gain((L, GLA_DV)),
        "w_out": nrm((L, MIX_WIDTH, D), MIX_WIDTH ** -0.5),
        "ffn2_norm": gain((L, D)),
        "ffn2_wg": nrm((L, D, D_FF), D ** -0.5),
        "ffn2_wu": nrm((L, D, D_FF), D ** -0.5),
        "ffn2_wd": nrm((L, D_FF, D), D_FF ** -0.5),
        "final_norm": gain((D,)),
    }


def reference(x, ffn1_norm, ffn1_wg, ffn1_wu, ffn1_wd, mix_norm, w_in, na_rpb, na_gain,
              w_gate_f, b_gate_f, w_gate_b, b_gate_b, gla_gain, w_out,
              ffn2_norm, ffn2_wg, ffn2_wu, ffn2_wd, final_norm):
    for l in range(DEPTH):
        x = hybrid_layer(x, ffn1_norm[l], ffn1_wg[l], ffn1_wu[l], ffn1_wd[l], mix_norm[l], w_in[l],
                         na_rpb[l], na_gain[l], w_gate_f[l], b_gate_f[l], w_gate_b[l], b_gate_b[l],
                         gla_gain[l], w_out[l], ffn2_norm[l], ffn2_wg[l], ffn2_wu[l], ffn2_wd[l])
    return rmsnorm(x, final_norm)
```

```python
import numpy as np
from contextlib import ExitStack
import concourse.bass as bass
import concourse.mybir as mybir
from concourse.bass_utils import run_bass_kernel_spmd

F32 = mybir.dt.float32
BF16 = mybir.dt.bfloat16
U8 = mybir.dt.uint8
AF = mybir.ActivationFunctionType
ALU = mybir.AluOpType

D = 1024
DFF = 2816
NFC = DFF // 128
NA_H = 8
GL_H = 4
INW = 3104
EPS = 1e-6
NEG = -30000.0
NDMA_SLOTS = 8


class Dep:
    __slots__ = ("key", "sem", "val", "owner")

    def __init__(self, key, sem, val, owner):
        self.key, self.sem, self.val, self.owner = key, sem, val, owner


class Res:
    def __init__(self, name="", ap=None):
        self.name = name
        self.ap = ap
        self.w = None
        self.r = {}
        self.children = []
        self.extra_w = []

    def sub(self, name):
        c = Res(name, self.ap)
        c.r = dict(self.r)
        c.w = self.w
        self.children.append(c)
        return c

    def merge_subs(self):
        best = {}
        for c in self.children:
            if c.w is not None and (c.w.key not in best or best[c.w.key].val < c.w.val):
                best[c.w.key] = c.w
        self.extra_w = list(best.values())

    def all_deps(self):
        ds = list(self.r.values()) + ([self.w] if self.w is not None else [])
        for c in self.children:
            ds += c.all_deps()
        return ds

    def __getitem__(self, k):
        return self.ap[k]


class Eng:
    def __init__(self, name, h, sem, key):
        self.name, self.h, self.sem, self.key = name, h, sem, key
        self.cnt = 0
        self.seen = {}
        self.slots = []
        self.ndma = 0


class Sched:
    def __init__(self, nc, es):
        self.nc = nc
        self.nkey = 0
        self.E = {}
        for name, h in (("pe", nc.tensor), ("act", nc.scalar), ("dve", nc.vector),
                        ("pool", nc.gpsimd), ("sp", nc.sync)):
            sem = es.enter_context(nc.semaphore("s_" + name))
            self.E[name] = Eng(name, h, sem, self._k())
        for name in ("sp", "pool"):
            e = self.E[name]
            for i in range(NDMA_SLOTS):
                e.slots.append((self._k(), es.enter_context(nc.semaphore("d_%s%d" % (name, i)))))

    def _k(self):
        self.nkey += 1
        return self.nkey

    def _wait(self, eng, d):
        if eng.seen.get(d.key, 0) >= d.val:
            return
        eng.h.wait_ge(d.sem, d.val)
        eng.seen[d.key] = d.val

    def _deps(self, en, reads, writes):
        eng = self.E[en]
        deps = {}

        def add(d):
            o = deps.get(d.key)
            if o is None or o.val < d.val:
                deps[d.key] = d

        for r in reads:
            if r.w is not None:
                add(r.w)
            for d in r.extra_w:
                add(d)
        for w in writes:
            if w.w is not None:
                add(w.w)
            for d in w.r.values():
                add(d)
        for d in deps.values():
            if d.owner == "pe" and en == "pe":
                continue
            self._wait(eng, d)

    def _commit(self, dep, reads, writes):
        for w in writes:
            w.w = dep
            w.r = {}
        for r in reads:
            if any(r is w for w in writes):
                continue
            o = r.r.get(dep.key)
            if o is None or o.val < dep.val:
                r.r[dep.key] = dep

    def op(self, en, fn, reads=(), writes=(), inc=True):
        eng = self.E[en]
        self._deps(en, reads, writes)
        ins = fn(eng.h)
        if inc:
            eng.cnt += 1
            ins.then_inc(eng.sem, 1)
            dep = Dep(eng.key, eng.sem, eng.cnt, en)
        else:
            dep = Dep(eng.key, eng.sem, eng.cnt + 1, en)
        self._commit(dep, reads, writes)
        return dep

    def dma(self, en, out, in_, reads=(), writes=()):
        eng = self.E[en]
        i = eng.ndma
        eng.ndma += 1
        key, sem = eng.slots[i % NDMA_SLOTS]
        rnd = i // NDMA_SLOTS
        self._deps(en, reads, writes)
        if rnd > 0:
            self._wait(eng, Dep(key, sem, 16 * rnd, "dma"))
        eng.h.dma_start(out=out, in_=in_).then_inc(sem, 16)
        dep = Dep(key, sem, 16 * (rnd + 1), "dma")
        self._commit(dep, reads, writes)
        return dep

    def finish(self):
        sp = self.E["sp"]
        for e in self.E.values():
            for j, (key, sem) in enumerate(e.slots):
                n = (e.ndma - j + NDMA_SLOTS - 1) // NDMA_SLOTS
                if n > 0:
                    self._wait(sp, Dep(key, sem, 16 * n, "dma"))
        for e in self.E.values():
            if e.name != "sp" and e.cnt > 0:
                self._wait(sp, Dep(e.key, e.sem, e.cnt, e.name))


def _esz(dt):
    return 2 if dt == BF16 else 4


class Arena:
    def __init__(self, nc, nbytes):
        self.t = nc.alloc_sbuf_tensor("arena", [128, nbytes], U8)
        self.size = nbytes
        self.top = 0
        self.live = []
        self.ghosts = []

    def mark(self):
        return (self.top, len(self.live))

    def release(self, m):
        top, n = m
        while len(self.live) > n:
            self.ghosts.append(self.live.pop())
        self.top = top

    def alloc(self, name, shape, dt):
        assert shape[0] == 128
        n = int(np.prod(shape[1:])) * _esz(dt)
        start = (self.top + 31) // 32 * 32
        end = start + n
        assert end <= self.size, "SBUF arena overflow at %s: need %d have %d" % (name, end, self.size)
        self.top = end
        ap = self.t[:, start:end].bitcast(dt)
        if len(shape) == 3:
            ap = ap.rearrange("p (a b) -> p a b", a=shape[1], b=shape[2])
        elif len(shape) == 4:
            ap = ap.rearrange("p (a b c) -> p a b c", a=shape[1], b=shape[2], c=shape[3])
        r = Res(name, ap)
        keep = []
        for (s, e, g) in self.ghosts:
            if s < end and e > start:
                for d in g.all_deps():
                    o = r.r.get(d.key)
                    if o is None or o.val < d.val:
                        r.r[d.key] = d
                if s >= start and e <= end:
                    continue
            keep.append((s, e, g))
        self.ghosts = keep
        self.live.append((start, end, r))
        return r


class DT:
    def __init__(self, nc, name, shape, dt, kind="Internal", ntile=None):
        self.h = nc.dram_tensor(name, list(shape), dt, kind=kind)
        self.ap = self.h.ap()
        nt = ntile if ntile is not None else max(1, shape[0] // 128)
        self.t = [Res("%s[%d]" % (name, i)) for i in range(nt)]

    def rows(self, i, n=1):
        return self.ap[i * 128:(i + n) * 128]


def na_geometry(S):
    R = S // 64
    wh = min(8, R)
    G = R // 4
    rs = lambda r: int(np.clip(r - wh // 2, 0, R - wh))
    cs = lambda c: int(np.clip(c - 8, 0, 64 - 16))
    colvalid = np.zeros((64, 64), bool)
    for c in range(64):
        colvalid[cs(c):cs(c) + 16, c] = True
    types = []
    tindex = {}
    groups = []
    for g in range(G):
        lo = rs(4 * g)
        hi = rs(4 * g + 3) + wh
        kb = lo - (lo % 2)
        ke = hi + (hi % 2)
        chunks = []
        for j in range((ke - kb) // 2):
            m = np.full((2, 64, 4, 64), NEG, np.float32)
            rel = [False, False]
            for b in range(2):
                kr = kb + 2 * j + b
                for a in range(4):
                    r = 4 * g + a
                    if rs(r) <= kr < rs(r) + wh:
                        m[b, :, a, :] = np.where(colvalid, 0.0, NEG)
                        rel[a // 2] = True
            e0 = 7 - (kb + 2 * j - 4 * g)
            key = (m.tobytes(), e0)
            if key not in tindex:
                tindex[key] = len(types)
                types.append((m.reshape(128, 256), e0))
            chunks.append(dict(kt=(kb + 2 * j) // 2, ty=tindex[key], rel=rel))
        groups.append(chunks)
    return groups, types


def host_consts(S):
    groups, types = na_geometry(S)
    c = {}
    c["ident"] = np.eye(128, dtype=np.float32)
    s = np.arange(128)
    g = -1.0 / 16.0
    c["gla_ltf"] = np.where(s[:, None] <= s[None, :], g, 0.0).astype(np.float32)
    c["gla_uf"] = np.where(s[:, None] > s[None, :], g, 0.0).astype(np.float32)
    c["gla_ltb"] = np.where(s[:, None] >= s[None, :], g, 0.0).astype(np.float32)
    c["gla_ub"] = np.where(s[:, None] < s[None, :], g, 0.0).astype(np.float32)
    mf = (s[:, None] <= s[None, :]).astype(np.float32)
    mb = (s[:, None] > s[None, :]).astype(np.float32)
    c["gla_mf"] = np.tile(mf, (1, 4))
    c["gla_mb"] = np.tile(mb, (1, 4))
    c["na_mask"] = np.stack([t[0] for t in types]).astype(np.float32)
    return c


def host_layout(inp, L):
    o = {}
    rpb = np.asarray(inp["na_rpb"], np.float32)
    cp = np.arange(64)[:, None]
    cc = np.arange(64)[None, :]
    dc = np.clip(cp - cc + 15, 0, 30)
    tr = np.zeros((L, NA_H, 64, 31, 64), np.float32)
    for e in range(15):
        tr[:, :, :, e + 8, :] = rpb[:, :, 14 - e, :][:, :, dc]
    o["na_tr"] = tr
    wg = np.zeros((L, 2, 17, 256), np.float32)
    wg[:, 0, :16] = inp["w_gate_f"]
    wg[:, 0, 16] = inp["b_gate_f"]
    wg[:, 1, :16] = inp["w_gate_b"]
    wg[:, 1, 16] = inp["b_gate_b"]
    o["wgate"] = wg
    return o


class Prog:
    def __init__(self, S, L, stop_after=None, dbg=()):
        self.S, self.L = S, L
        self.NT = S // 128
        self.NB = S // 512
        self.stop_after = stop_after
        self.dbg = set(dbg)
        nc = self.nc = bass.Bass("TRN2", target_bir_lowering=False)
        self.consts = host_consts(S)
        self.groups, self.types = na_geometry(S)
        ein = lambda n, sh: nc.dram_tensor(n, list(sh), F32, kind="ExternalInput").ap()
        I = self.I = {}
        I["x"] = DT(nc, "x", [S, D], F32, kind="ExternalInput")
        for n, sh in (("ffn1_norm", [L, D]), ("ffn1_wg", [L, D, DFF]), ("ffn1_wu", [L, D, DFF]),
                      ("ffn1_wd", [L, DFF, D]), ("mix_norm", [L, D]), ("w_in", [L, D, INW]),
                      ("na_gain", [L, 512]), ("gla_gain", [L, 128]), ("w_out", [L, D, D]),
                      ("ffn2_norm", [L, D]), ("ffn2_wg", [L, D, DFF]), ("ffn2_wu", [L, D, DFF]),
                      ("ffn2_wd", [L, DFF, D]), ("final_norm", [1, D]),
                      ("na_tr", [L, NA_H, 64, 31, 64]), ("wgate", [L, 2, 17, 256])):
            I[n] = ein(n, sh)
        for n, v in self.consts.items():
            I[n] = ein(n, v.shape)
        self.out = DT(nc, "out", [S, D], F32, kind="ExternalOutput")
        self.xs = DT(nc, "xs", [S, D], F32, kind=self._kind("xs"))
        self.nao = DT(nc, "nao", [S, 512], F32, kind=self._kind("nao"))
        self.glo = DT(nc, "glo", [S, 512], BF16, kind=self._kind("glo"))
        self.ofw = DT(nc, "ofw", [S, 512], F32, kind=self._kind("ofw"))
        self.grs = DT(nc, "grs", [S, 512], BF16, kind=self._kind("grs"))
        self.spd = DT(nc, "spd", [S, 512], F32, kind=self._kind("spd"))

    def _kind(self, n):
        return "ExternalOutput" if n in self.dbg else "Internal"

    def dump(self, name, res, ap=None):
        if name not in self.dbg:
            return
        ap = res.ap if ap is None else ap
        d = self.nc.dram_tensor("dbg_" + name, list(ap.shape), ap.dtype, kind="ExternalOutput").ap()
        self.s.dma("sp", d, ap, reads=[res])

    def bcast_row(self, ap_row, n):
        return ap_row.partition_broadcast(128)

    def build(self):
        nc = self.nc
        with ExitStack() as es:
            es.enter_context(nc.allow_low_precision("bf16 matmul operands, fp32 accumulation"))
            es.enter_context(nc.allow_non_contiguous_dma(reason="tiled layouts"))
            self.s = Sched(nc, es)
            self.A = Arena(nc, 212480)
            self.P = []
            for i in range(8):
                t = nc.alloc_psum_tensor("bank%d" % i, [128, 512], F32)
                self.P.append(Res("bank%d" % i, t[:, :]))
            A, s = self.A, self.s
            self.ident = A.alloc("ident", [128, 128], BF16)
            s.dma("pool", self.ident.ap, self.I["ident"], writes=[self.ident])
            self.mhalf = A.alloc("mhalf", [128, 8], F32)
            s.op("pool", lambda e: e.memset(self.mhalf.ap, -0.5), writes=[self.mhalf])
            self.body()
            s.finish()
        return nc

    def body(self):
        src = self.I["x"]
        for l in range(self.L):
            self.ffn(l, "ffn1", src, self.xs)
            src = self.xs
            if self.stop_after == ("ffn1", l):
                break
            mb = self.A.mark()
            pre = self.mixer(l)
            if self.stop_after is not None and self.stop_after[1] == l and \
                    self.stop_after[0] in ("mix", "proj_na", "na", "gla", "proj_gla"):
                self.A.release(mb)
                break
            self.ffn(l, "ffn2", self.xs, self.xs, pre=pre)
            self.A.release(mb)
            if self.stop_after == ("ffn2", l):
                break
        self.final_norm(src)

    def load_gain(self, name, l, n):
        A, s = self.A, self.s
        g = A.alloc("gain_" + name, [128, n], F32)
        s.dma("sp", g.ap, self.I[name][l:l + 1, :].partition_broadcast(128), writes=[g])
        return g

    def norm_tiles(self, xts, gain, hbs, ss, junk, n=D):
        s = self.s
        k = len(xts)
        s.op("dve", lambda e: e.memset(ss[:, 0:8], 0.0), writes=[ss])
        for j, (xr, xa) in enumerate(xts):
            s.op("act", lambda e, xa=xa, j=j: e.activation(out=junk[:, 0:n], in_=xa, func=AF.Square,
                                                          accum_out=ss[:, j:j + 1]),
                 reads=[xr], writes=[junk, ss])
        s.op("dve", lambda e: e.tensor_scalar(out=ss[:, 8:8 + k], in0=ss[:, 0:k], scalar1=1.0 / n, scalar2=EPS,
                                              op0=ALU.mult, op1=ALU.add), reads=[ss], writes=[ss])
        s.op("act", lambda e: e.activation(out=ss[:, 8:8 + k], in_=ss[:, 8:8 + k], func=AF.Sqrt),
             reads=[ss], writes=[ss])
        s.op("dve", lambda e: e.reciprocal(out=ss[:, 16:16 + k], in_=ss[:, 8:8 + k]), reads=[ss], writes=[ss])
        for j, (xr, xa) in enumerate(xts):
            hb = hbs[j]
            s.op("dve", lambda e, xa=xa, j=j, hb=hb: e.scalar_tensor_tensor(
                out=hb[:, 0:n], in0=xa, scalar=ss[:, 16 + j:17 + j], in1=gain[:, 0:n],
                op0=ALU.mult, op1=ALU.mult), reads=[xr, ss, gain], writes=[hb])

    def transpose_to(self, hb, bank, dst, dst_ap, nchunk=8):
        s = self.s
        pb = bank.ap.bitcast(BF16)
        for c in range(nchunk):
            s.op("pe", lambda e, c=c: e.transpose(out=pb[:, c * 128:(c + 1) * 128], in_=hb[:, c * 128:(c + 1) * 128],
                                                 identity=self.ident.ap),
                 reads=[hb, self.ident], writes=[bank], inc=(c == nchunk - 1))
        s.op("act", lambda e: e.copy(out=dst_ap, in_=pb[:, 0:nchunk * 128].rearrange("p (c t) -> p c t", c=nchunk)),
             reads=[bank], writes=[dst])

    def ffn(self, l, which, src, dst, pre=None):
        A, s, I, P = self.A, self.s, self.I, self.P
        NB = self.NB
        m0 = A.mark()
        if pre is None:
            pre = self.ffn_alloc_gu(l, which)
        wg, wu, wg_q, wu_q = pre["wg"], pre["wu"], pre["wg_q"], pre["wu_q"]
        wd = A.alloc("wd", [128, NFC, D], BF16)
        wd_q = [wd.sub("wdq%d" % q) for q in range(2)]
        FQ = DFF // 4
        dsrc = I[which + "_wd"][l].rearrange("(c p) d -> p c d", p=128)
        gain = self.load_gain(which + "_norm", l, D)
        if not pre["loaded"]:
            self.ffn_load_gu(pre)
        for q in range(2):
            s.dma("pool", wd[:, q * 11:(q + 1) * 11, :], dsrc[:, q * 11:(q + 1) * 11, :], writes=[wd_q[q]])
        xn = [A.alloc("xn%d" % i, [128, D], F32) for i in range(4)]
        xr = [A.alloc("xr%d" % i, [128, D], F32) for i in range(2)]
        hb = [A.alloc("hb%d" % i, [128, D], BF16) for i in range(4)]
        hT = [A.alloc("hT%d" % i, [128, 8, 512], BF16) for i in range(2)]
        aT = A.alloc("aT", [128, NFC, 512], BF16)
        sg = [A.alloc("sg0", [128, 512], BF16)] * 2
        ss = A.alloc("ss", [128, 32], F32)
        Tb = [P[0], P[1]]
        Gb = [P[2], P[3]]
        Ub = [P[4], P[5]]
        Yb = [P[6], P[7]]

        def norm_act(b, q):
            s.op("dve", lambda e: e.memset(ss[:, 0:8], 0.0), writes=[ss])
            for j in range(4):
                t = b * 4 + j
                s.dma("sp", xn[j].ap, src.rows(t), reads=[src.t[t]], writes=[xn[j]])
                s.op("act", lambda e, j=j: e.activation(out=hb[j].ap, in_=xn[j].ap, func=AF.Square,
                                                       accum_out=ss[:, j:j + 1]),
                     reads=[xn[j]], writes=[hb[j], ss])

        def norm_dve(b, q):
            s.op("dve", lambda e: e.tensor_scalar(out=ss[:, 8:12], in0=ss[:, 0:4], scalar1=1.0 / D, scalar2=EPS,
                                                  op0=ALU.mult, op1=ALU.add), reads=[ss], writes=[ss])
            s.op("pool", lambda e: e.tensor_tensor(out=ss[:, 16:20], in0=ss[:, 8:12], in1=self.mhalf[:, 0:4], op=ALU.pow),
                 reads=[ss, self.mhalf], writes=[ss])
            for j in range(4):
                s.op("dve", lambda e, j=j: e.scalar_tensor_tensor(
                    out=hb[j].ap, in0=xn[j].ap, scalar=ss[:, 16 + j:17 + j], in1=gain.ap,
                    op0=ALU.mult, op1=ALU.mult), reads=[xn[j], ss, gain], writes=[hb[j]])

        def norm_T(b, q):
            for j in range(4):
                self.transpose_to(hb[j], Tb[j % 2], hT[q % 2], hT[q % 2][:, :, j * 128:(j + 1) * 128])

        def stage_gu(b, q):
            h = hT[q % 2]
            for fc in range(NFC):
                q = fc * 128 // FQ
                q2 = (fc * 128 + 127) // FQ
                gq = [wg_q[q]] + ([wg_q[q2]] if q2 != q else [])
                uq = [wu_q[q]] + ([wu_q[q2]] if q2 != q else [])
                G, U = Gb[fc % 2], Ub[fc % 2]
                for c in range(8):
                    s.op("pe", lambda e, c=c, fc=fc, G=G: e.matmul(G.ap, lhsT=wg[:, c, fc * 128:(fc + 1) * 128],
                                                                  rhs=h[:, c, :], start=(c == 0), stop=(c == 7)),
                         reads=[h] + gq, writes=[G], inc=(c == 7))
                for c in range(8):
                    s.op("pe", lambda e, c=c, fc=fc, U=U: e.matmul(U.ap, lhsT=wu[:, c, fc * 128:(fc + 1) * 128],
                                                                  rhs=h[:, c, :], start=(c == 0), stop=(c == 7)),
                         reads=[h] + uq, writes=[U], inc=(c == 7))
                sgb = sg[fc % 2]
                s.op("act", lambda e, G=G, sgb=sgb: e.activation(out=sgb.ap, in_=G.ap, func=AF.Silu),
                     reads=[G], writes=[sgb])
                s.op("dve", lambda e, U=U, sgb=sgb, fc=fc: e.tensor_tensor(out=aT[:, fc, :], in0=U.ap, in1=sgb.ap,
                                                                          op=ALU.mult),
                     reads=[U, sgb], writes=[aT])

        def stage_down(b, hooks):
            for j in range(4):
                if j in hooks:
                    hooks[j]()
                t = b * 4 + j
                xt = xr[j % 2]
                s.dma("sp", xt.ap, src.rows(t), reads=[src.t[t]], writes=[xt])
                for hf in range(2):
                    Y = Yb[hf]
                    for fc in range(NFC):
                        s.op("pe", lambda e, fc=fc, hf=hf, j=j, Y=Y: e.matmul(
                            Y.ap, lhsT=aT[:, fc, j * 128:(j + 1) * 128], rhs=wd[:, fc, hf * 512:(hf + 1) * 512],
                            start=(fc == 0), stop=(fc == NFC - 1)),
                             reads=[aT, wd_q[fc // 11]], writes=[Y], inc=(fc == NFC - 1))
                    s.op("dve", lambda e, hf=hf, Y=Y, xt=xt: e.scalar_tensor_tensor(
                        out=xt[:, hf * 512:(hf + 1) * 512], in0=Y.ap, scalar=0.5, in1=xt[:, hf * 512:(hf + 1) * 512],
                        op0=ALU.mult, op1=ALU.add), reads=[Y, xt], writes=[xt])
                s.dma("pool", dst.rows(t), xt.ap, reads=[xt], writes=[dst.t[t]])

        border = list(range(NB))
        if which == "ffn2" and getattr(self, "block_order", None):
            border = list(self.block_order)
        norm_act(border[0], 0)
        norm_dve(border[0], 0)
        norm_T(border[0], 0)
        for i, b in enumerate(border):
            stage_gu(b, i)
            if i + 1 < NB:
                nb = border[i + 1]
                norm_act(nb, i + 1)
                stage_down(b, {1: lambda nb=nb, i=i: norm_dve(nb, i + 1), 2: lambda nb=nb, i=i: norm_T(nb, i + 1)})
            else:
                stage_down(b, {})
        A.release(m0)

    def final_norm(self, src):
        A, s = self.A, self.s
        m0 = A.mark()
        gain = A.alloc("gain_fin", [128, D], F32)
        s.dma("sp", gain.ap, self.I["final_norm"][0:1, :].partition_broadcast(128), writes=[gain])
        xt = [A.alloc("fx%d" % i, [128, D], F32) for i in range(3)]
        ot = [A.alloc("fo%d" % i, [128, D], F32) for i in range(3)]
        junk = A.alloc("fjunk", [128, D], BF16)
        ss = A.alloc("fss", [128, 32], F32)
        for t in range(self.NT):
            x = xt[t % 3]
            o = ot[t % 3]
            s.dma("sp", x.ap, src.rows(t), reads=[src.t[t]], writes=[x])
            s.op("dve", lambda e: e.memset(ss[:, 0:1], 0.0), writes=[ss])
            s.op("act", lambda e, x=x: e.activation(out=junk.ap, in_=x.ap, func=AF.Square, accum_out=ss[:, 0:1]),
                 reads=[x], writes=[junk, ss])
            s.op("dve", lambda e: e.tensor_scalar(out=ss[:, 8:9], in0=ss[:, 0:1], scalar1=1.0 / D, scalar2=EPS,
                                                  op0=ALU.mult, op1=ALU.add), reads=[ss], writes=[ss])
            s.op("pool", lambda e: e.tensor_tensor(out=ss[:, 16:17], in0=ss[:, 8:9], in1=self.mhalf[:, 0:1], op=ALU.pow),
                 reads=[ss, self.mhalf], writes=[ss])
            s.op("dve", lambda e, x=x, o=o: e.scalar_tensor_tensor(out=o.ap, in0=x.ap, scalar=ss[:, 16:17], in1=gain.ap,
                                                                  op0=ALU.mult, op1=ALU.mult),
                 reads=[x, ss, gain], writes=[o])
            s.dma("pool", self.out.rows(t), o.ap, reads=[o], writes=[self.out.t[t]])
        A.release(m0)

    def mixer(self, l):
        A, s, I = self.A, self.s, self.I
        mb = A.mark()
        WB = A.alloc("winB", [128, 8, 1568], BF16)
        wsrc = I["w_in"][l].rearrange("(c p) f -> p c f", p=128)
        WBq = [WB.sub("winB%d" % i) for i in range(4)]
        self.winB_bounds = [(0, 512), (512, 1024), (1024, 1536), (1536, 1568)]
        m0 = A.mark()
        res = self.proj_na(l, lambda: [s.dma("pool", WB[:, :, lo:hi], wsrc[:, :, 1536 + lo:1536 + hi], writes=[WBq[i]])
                                       for i, (lo, hi) in enumerate(self.winB_bounds)])
        if self.stop_after != ("proj_na", l):
            self.na(l, *res)
        A.release(m0)
        if self.stop_after in (("na", l), ("proj_na", l)):
            return None
        res = self.proj_gla(l, WB, WBq)
        if self.stop_after != ("proj_gla", l):
            self.gla(l, res)
        A.release(mb)
        if self.stop_after in (("gla", l), ("proj_gla", l)):
            return None
        pre = self.ffn_alloc_gu(l, "ffn2")
        self.wout(l, pre)
        return pre

    def ffn_alloc_gu(self, l, which):
        A = self.A
        wg = A.alloc("wg", [128, 8, DFF], BF16)
        wu = A.alloc("wu", [128, 8, DFF], BF16)
        wg_q = [wg.sub("wgq%d" % q) for q in range(4)]
        wu_q = [wu.sub("wuq%d" % q) for q in range(4)]
        return dict(wg=wg, wu=wu, wg_q=wg_q, wu_q=wu_q, loaded=False, l=l, which=which)

    def ffn_load_gu(self, pre):
        s, I = self.s, self.I
        l, which = pre["l"], pre["which"]
        FQ = DFF // 4
        gsrc = I[which + "_wg"][l].rearrange("(c p) f -> p c f", p=128)
        usrc = I[which + "_wu"][l].rearrange("(c p) f -> p c f", p=128)
        for q in range(4):
            s.dma("pool", pre["wg"][:, :, q * FQ:(q + 1) * FQ], gsrc[:, :, q * FQ:(q + 1) * FQ], writes=[pre["wg_q"][q]])
            s.dma("pool", pre["wu"][:, :, q * FQ:(q + 1) * FQ], usrc[:, :, q * FQ:(q + 1) * FQ], writes=[pre["wu_q"][q]])
        pre["loaded"] = True

    def proj_blocks(self, l, emit):
        A, s, P = self.A, self.s, self.P
        gain = self.load_gain("mix_norm", l, D)
        xn = [A.alloc("pxn%d" % i, [128, D], F32) for i in range(4)]
        hb = [A.alloc("phb%d" % i, [128, D], BF16) for i in range(4)]
        hT = [A.alloc("phT%d" % i, [128, 8, 512], BF16) for i in range(2)]
        ss = A.alloc("pss", [128, 32], F32)
        Tb = [P[0], P[1]]

        ssq = [ss.sub("c%d" % j) for j in range(4)]

        def frontA(b):
            s.op("dve", lambda e: e.memset(ss[:, 0:8], 0.0), writes=[ss] + ssq)
            for j in range(4):
                t = b * 4 + j
                s.dma("sp", xn[j].ap, self.xs.rows(t), reads=[self.xs.t[t]], writes=[xn[j]])
                s.op("act", lambda e, j=j: e.activation(out=hb[j].ap, in_=xn[j].ap, func=AF.Square,
                                                       accum_out=ss[:, j:j + 1]),
                     reads=[xn[j]], writes=[hb[j], ssq[j]])
            s.op("dve", lambda e: e.tensor_scalar(out=ss[:, 8:12], in0=ss[:, 0:4], scalar1=1.0 / D, scalar2=EPS,
                                                  op0=ALU.mult, op1=ALU.add), reads=[ss] + ssq, writes=[ss])
            s.op("pool", lambda e: e.tensor_tensor(out=ss[:, 16:20], in0=ss[:, 8:12], in1=self.mhalf[:, 0:4], op=ALU.pow),
                 reads=[ss, self.mhalf], writes=[ss])
            for j in range(4):
                s.op("dve", lambda e, j=j: e.scalar_tensor_tensor(
                    out=hb[j].ap, in0=xn[j].ap, scalar=ss[:, 16 + j:17 + j], in1=gain.ap,
                    op0=ALU.mult, op1=ALU.mult), reads=[xn[j], ss, gain], writes=[hb[j]])

        def frontB(b):
            h = hT[b % 2]
            for j in range(4):
                self.transpose_to(hb[j], Tb[j % 2], h, h[:, :, j * 128:(j + 1) * 128])

        frontA(0)
        frontB(0)
        for b in range(self.NB):
            if b + 1 < self.NB:
                frontA(b + 1)
                emit(b, hT[b % 2], lambda b=b: frontB(b + 1))
            else:
                emit(b, hT[b % 2], lambda: None)

    def proj_na(self, l, after_w=None):
        A, s, P, I = self.A, self.s, self.P, self.I
        S, NT = self.S, self.NT
        QT = A.alloc("QT", [128, 4, S], BF16)
        KT = A.alloc("KT", [128, 4, S], BF16)
        Vx = A.alloc("Vx", [128, NT, 8, 65], BF16)
        s.op("pool", lambda e: e.memset(Vx.ap.rearrange("p t h d -> p (t h d)"), 1.0), writes=[Vx])
        m1 = A.mark()
        W = A.alloc("winA", [128, 8, 1536], BF16)
        wsrc = I["w_in"][l].rearrange("(c p) f -> p c f", p=128)
        Wq = [W.sub("winA%d" % i) for i in range(3)]
        for i in range(3):
            s.dma("pool", W[:, :, i * 512:(i + 1) * 512], wsrc[:, :, i * 512:(i + 1) * 512], writes=[Wq[i]])
        if after_w is not None:
            after_w()
        FM = [P[2], P[3], P[4], P[5]]
        TM = [P[6], P[7]]
        cnt = [0, 0]

        def emit(b, h, mid):
            for i in range(8):
                bank = FM[cnt[0] % 4]
                cnt[0] += 1
                for c in range(8):
                    s.op("pe", lambda e, c=c, i=i, bank=bank: e.matmul(bank.ap, lhsT=W[:, c, i * 128:(i + 1) * 128],
                                                                      rhs=h[:, c, :], start=(c == 0), stop=(c == 7)),
                         reads=[h, Wq[i // 4]], writes=[bank], inc=(c == 7))
                if i < 4:
                    s.op("act", lambda e, i=i, bank=bank: e.mul(out=QT[:, i, b * 512:(b + 1) * 512], in_=bank.ap, mul=0.125),
                         reads=[bank], writes=[QT.sub("e")])
                else:
                    s.op("dve", lambda e, i=i, bank=bank: e.tensor_copy(out=KT[:, i - 4, b * 512:(b + 1) * 512], in_=bank.ap),
                         reads=[bank], writes=[KT.sub("e")])
            mid()
            for j in range(4):
                t = b * 4 + j
                bank = TM[cnt[1] % 2]
                cnt[1] += 1
                for c in range(8):
                    s.op("pe", lambda e, c=c, j=j, bank=bank: e.matmul(bank.ap, lhsT=h[:, c, j * 128:(j + 1) * 128],
                                                                      rhs=W[:, c, 1024:1536], start=(c == 0), stop=(c == 7)),
                         reads=[h, Wq[2]], writes=[bank], inc=(c == 7))
                s.op("dve", lambda e, t=t, bank=bank: e.tensor_copy(
                    out=Vx[:, t, :, 0:64], in_=bank.ap.rearrange("p (h d) -> p h d", h=8)),
                     reads=[bank], writes=[Vx.sub("e")])

        self.proj_blocks(l, emit)
        for r_ in (QT, KT, Vx):
            r_.merge_subs()
        A.release(m1)
        self.dump("QT", QT)
        self.dump("winA", W)
        self.dump("KT", KT)
        self.dump("Vx", Vx)
        return QT, KT, Vx

    def na(self, l, QT, KT, Vx):
        A, s, P, I = self.A, self.s, self.P, self.I
        types, groups = self.types, self.groups
        NTY = len(types)
        maskS = A.alloc("na_mask", [128, NTY, 256], BF16)
        s.dma("pool", maskS.ap, I["na_mask"].rearrange("t p q -> p t q"), writes=[maskS])
        TbS = [A.alloc("tbs%d" % i, [128, 31, 64], F32) for i in range(2)]
        biasT = [A.alloc("biasT%d" % i, [128, NTY, 256], BF16) for i in range(2)]
        btf = A.alloc("btf", [128, NTY, 256], F32)
        PT = [A.alloc("pt%d" % i, [128, 512], BF16) for i in range(3)]
        PT2 = [A.alloc("ptm%d" % i, [128, 512], BF16) for i in range(3)]
        ost = [A.alloc("ost%d" % i, [128, 2, 64], F32) for i in range(2)]
        rec = A.alloc("narec", [128, 4], F32)
        for tb in TbS:
            s.op("pool", lambda e, tb=tb: e.memset(tb.ap.rearrange("p e c -> p (e c)"), 0.0), writes=[tb])
        Ob = [[P[2], P[3]], [P[4], P[5]]]

        def build(h):
            tb, bt = TbS[h % 2], biasT[h % 2]
            s.dma("sp", tb[0:64, :, :], I["na_tr"][l, h], writes=[tb])
            s.dma("sp", tb[64:128, 1:31, :], I["na_tr"][l, h, :, 0:30, :], writes=[tb])
            tb2 = tb.ap.rearrange("p e c -> p (e c)")
            for ty, (m, e0) in enumerate(types):
                off = (e0 + 8) * 64
                s.op("pool", lambda e, ty=ty, off=off: e.tensor_tensor(out=btf[:, ty, :], in0=tb2[:, off:off + 256],
                                                                      in1=maskS[:, ty, :], op=ALU.add),
                     reads=[tb, maskS], writes=[btf])

        def build_b(h):
            bt = biasT[h % 2]
            s.op("act", lambda e: e.activation(out=bt.ap, in_=btf.ap, func=AF.Exp), reads=[btf], writes=[bt])

        STb = [P[0], P[1], P[6], P[7]]
        NPT = 5
        PT = PT + [A.alloc("pt3", [128, 512], BF16), A.alloc("pt4", [128, 512], BF16)]
        PT2 = PT2 + [A.alloc("ptm3", [128, 512], BF16), A.alloc("ptm4", [128, 512], BF16)]
        PT2h = [[p_.sub("h0"), p_.sub("h1")] for p_ in PT2]
        osth = [[o_.sub("a0"), o_.sub("a1")] for o_ in ost]
        rech = [rec.sub("a0"), rec.sub("a1")]
        build(0)
        build_b(0)
        self.dump("biasT", biasT[0])
        self.dump("tbs", TbS[0])
        items = []
        ngrp = 0
        for h in range(NA_H):
            for g, chunks in enumerate(groups):
                lastidx = [max(j for j, c in enumerate(chunks) if c["rel"][a2]) for a2 in range(2)]
                first = [True, True]
                npairs = (len(chunks) + 1) // 2
                for pi in range(npairs):
                    pair = chunks[2 * pi:2 * pi + 2]
                    pv = []
                    for u, ch in enumerate(pair):
                        jj = 2 * pi + u
                        for a2 in range(2):
                            if ch["rel"][a2]:
                                pv.append((u, ch, a2, first[a2], jj == lastidx[a2]))
                                first[a2] = False
                    items.append(dict(h=h, g=g, pair=pair, pv=pv, gi=ngrp, newhead=(g == 0 and pi == 0),
                                      lastpair=(pi == npairs - 1), midhead=(g == len(groups) // 2 and pi == 0)))
                ngrp += 1

        def front(n):
            it = items[n]
            h, g, pair = it["h"], it["g"], it["pair"]
            if it["newhead"] and h + 1 < NA_H:
                build(h + 1)
            if it["midhead"] and h + 1 < NA_H:
                build_b(h + 1)
            bt = biasT[h % 2]
            i, a = h // 2, h % 2
            pl = slice(64 * a, 64 * a + 64)
            q0 = g * 256
            ST = STb[n % 4]
            pt0, pt = PT[n % NPT], PT2[n % NPT]
            pth = PT2h[n % NPT]
            for u, ch in enumerate(pair):
                kt = ch["kt"]
                s.op("pe", lambda e, u=u, kt=kt: e.matmul(
                    ST[:, u * 256:(u + 1) * 256], lhsT=KT[pl, i, kt * 128:(kt + 1) * 128],
                    rhs=QT[pl, i, q0:q0 + 256], start=True, stop=True),
                     reads=[KT, QT], writes=[ST], inc=(u == len(pair) - 1))
            w = 256 * len(pair)
            s.op("act", lambda e: e.activation(out=pt0[:, 0:w], in_=ST[:, 0:w], func=AF.Exp), reads=[ST], writes=[pt0])
            for u, ch in enumerate(pair):
                s.op("dve", lambda e, u=u, ch=ch: e.tensor_tensor(
                    out=pt[:, u * 256:(u + 1) * 256], in0=pt0[:, u * 256:(u + 1) * 256], in1=bt[:, ch["ty"], :],
                    op=ALU.mult), reads=[pt0, bt], writes=[pth[u]])

        def back(n):
            it = items[n]
            h, g = it["h"], it["g"]
            q0 = g * 256
            O = Ob[it["gi"] % 2]
            os_ = ost[it["gi"] % 2]
            pt = PT2[n % NPT]
            pth = PT2h[n % NPT]
            osh = osth[it["gi"] % 2]
            for (u, ch, a2, st, sp_) in it["pv"]:
                s.op("pe", lambda e, u=u, ch=ch, a2=a2, st=st, sp_=sp_: e.matmul(
                    O[a2][:, 0:65], lhsT=pt[:, u * 256 + a2 * 128:u * 256 + a2 * 128 + 128],
                    rhs=Vx[:, ch["kt"], h, :], start=st, stop=sp_),
                     reads=[pth[u], Vx], writes=[O[a2]], inc=sp_)
            if it["lastpair"]:
                for a2 in range(2):
                    s.op("dve", lambda e, a2=a2: e.reciprocal(out=rec[:, a2:a2 + 1], in_=O[a2][:, 64:65]),
                         reads=[O[a2]], writes=[rech[a2]])
                    s.op("dve", lambda e, a2=a2: e.tensor_scalar_mul(out=os_[:, a2, :], in0=O[a2][:, 0:64],
                                                                    scalar1=rec[:, a2:a2 + 1]),
                         reads=[O[a2], rech[a2]], writes=[osh[a2]])
                dst = self.nao.ap[q0:q0 + 256, h * 64:(h + 1) * 64].rearrange("(a p) d -> p a d", p=128)
                s.dma("sp", dst, os_.ap, reads=[osh[0], osh[1]], writes=[self.nao.t[2 * g], self.nao.t[2 * g + 1]])

        LOOK = 3
        for n in range(len(items) + LOOK):
            if n < len(items):
                front(n)
            if n >= LOOK:
                back(n - LOOK)

    def proj_gla(self, l, W, Wq):
        A, s, P, I = self.A, self.s, self.P, self.I
        S, NT = self.S, self.NT
        R = {}
        R["gqT"] = A.alloc("gqT", [128, 2, S], BF16)
        R["gkT"] = A.alloc("gkT", [128, 2, S], BF16)
        R["gk"] = A.alloc("gk", [128, NT, 256], BF16)
        R["gv"] = A.alloc("gv", [128, NT, 512], BF16)
        m1 = A.mark()
        wgt = A.alloc("wgt", [128, 2, 256], F32)
        for d in range(2):
            s.dma("sp", wgt[0:17, d, :], I["wgate"][l, d], writes=[wgt])
        codeT = [[A.alloc("codeT%d%d" % (d, i), [128, 512], F32) for i in range(2)] for d in range(2)]
        for d in range(2):
            for i in range(2):
                s.op("pool", lambda e, d=d, i=i: e.memset(codeT[d][i][0:32, :], 1.0), writes=[codeT[d][i]])
        e1 = [A.alloc("ge1%d" % i, [128, 512], F32) for i in range(2)]
        grst = [A.alloc("grst%d" % i, [128, 512], BF16) for i in range(2)]
        sil = [A.alloc("sil%d" % i, [128, 512], F32) for i in range(4)]
        FM = [P[2], P[3]]
        TM = [P[4], P[5]]
        GBs = [P[6], P[7]]
        cnt = [0, 0]
        rsb = [A.alloc("rsb%d" % i, [128, 512], F32) for i in range(2)]
        gqT, gkT, gk, gv = R["gqT"], R["gkT"], R["gk"], R["gv"]
        spst = [A.alloc("spst%d" % i, [128, 512], F32) for i in range(2)]

        def emit(b, h, mid):
            for i in range(4):
                bank = FM[cnt[0] % 2]
                cnt[0] += 1
                for c in range(8):
                    s.op("pe", lambda e, c=c, i=i, bank=bank: e.matmul(bank.ap, lhsT=W[:, c, i * 128:(i + 1) * 128],
                                                                      rhs=h[:, c, :], start=(c == 0), stop=(c == 7)),
                         reads=[h, Wq[0]], writes=[bank], inc=(c == 7))
                if i < 2:
                    s.op("act", lambda e, i=i, bank=bank: e.mul(out=gqT[:, i, b * 512:(b + 1) * 512], in_=bank.ap, mul=0.125),
                         reads=[bank], writes=[gqT.sub("e")])
                else:
                    s.op("dve", lambda e, i=i, bank=bank: e.tensor_copy(out=gkT[:, i - 2, b * 512:(b + 1) * 512], in_=bank.ap),
                         reads=[bank], writes=[gkT.sub("e")])
            for d in range(2):
                bank = FM[cnt[0] % 2]
                cnt[0] += 1
                ct = codeT[d][b % 2]
                for c in range(8):
                    s.op("pe", lambda e, c=c, d=d, bank=bank: e.matmul(bank[0:16, :], lhsT=W[:, c, 1536 + 16 * d:1552 + 16 * d],
                                                                      rhs=h[:, c, :], start=(c == 0), stop=(c == 7)),
                         reads=[h, Wq[3]], writes=[bank], inc=(c == 7))
                s.op("dve", lambda e, bank=bank, ct=ct: e.tensor_copy(out=ct[0:16, :], in_=bank[0:16, :]),
                     reads=[bank], writes=[ct])
            mid()
            for j in range(4):
                t = b * 4 + j
                GB = GBs[t % 2]
                for d in range(2):
                    ct = codeT[d][b % 2]
                    s.op("pe", lambda e, d=d, j=j, ct=ct: e.matmul(GB[:, d * 256:(d + 1) * 256],
                                                                  lhsT=ct[0:17, j * 128:(j + 1) * 128], rhs=wgt[0:17, d, :],
                                                                  start=True, stop=True),
                         reads=[ct, wgt], writes=[GB], inc=(d == 1))
                ee = e1[t % 2]
                s.op("act", lambda e, ee=ee: e.activation(out=ee.ap, in_=GB.ap, func=AF.Exp, scale=-1.0),
                     reads=[GB], writes=[ee])
                st = spst[t % 2]
                s.op("act", lambda e, ee=ee, st=st: e.activation(out=st.ap, in_=ee.ap, func=AF.Ln, bias=1.0),
                     reads=[ee], writes=[st])
                s.dma("sp", self.spd.rows(t), st.ap, reads=[st], writes=[self.spd.t[t]])
                for (lo, hi, kind) in ((256, 512, "k"), (512, 1024, "v"), (1024, 1536, "r")):
                    bank = TM[cnt[1] % 2]
                    cnt[1] += 1
                    n = hi - lo
                    for c in range(8):
                        s.op("pe", lambda e, c=c, j=j, bank=bank, lo=lo, hi=hi, n=n: e.matmul(
                            bank[:, 0:n], lhsT=h[:, c, j * 128:(j + 1) * 128], rhs=W[:, c, lo:hi],
                            start=(c == 0), stop=(c == 7)),
                             reads=[h, Wq[lo // 512]], writes=[bank], inc=(c == 7))
                    if kind == "k":
                        s.op("dve", lambda e, t=t, bank=bank: e.tensor_copy(out=gk[:, t, :], in_=bank[:, 0:256]),
                             reads=[bank], writes=[gk.sub("e")])
                    elif kind == "v":
                        s.op("dve", lambda e, t=t, bank=bank: e.tensor_copy(out=gv[:, t, :], in_=bank.ap),
                             reads=[bank], writes=[gv.sub("e")])
                    else:
                        gs = grst[t % 2]
                        sa, sb = sil[2 * (t % 2)], sil[2 * (t % 2) + 1]
                        rs = rsb[t % 2]
                        s.op("dve", lambda e, rs=rs, bank=bank: e.tensor_copy(out=rs.ap, in_=bank.ap), reads=[bank], writes=[rs])
                        s.op("act", lambda e, rs=rs: e.activation(out=sa.ap, in_=rs.ap, func=AF.Exp, scale=-1.0),
                             reads=[rs], writes=[sa])
                        s.op("act", lambda e: e.activation(out=sb.ap, in_=sa.ap, func=AF.Ln, bias=1.0),
                             reads=[sa], writes=[sb])
                        s.op("act", lambda e: e.activation(out=sa.ap, in_=sb.ap, func=AF.Exp, scale=-1.0),
                             reads=[sb], writes=[sa])
                        s.op("pool", lambda e, gs=gs, rs=rs: e.tensor_tensor(out=gs.ap, in0=rs.ap, in1=sa.ap, op=ALU.mult),
                             reads=[rs, sa], writes=[gs])
                        s.dma("sp", self.grs.rows(t), gs.ap, reads=[gs], writes=[self.grs.t[t]])

        self.proj_blocks(l, emit)
        for r_ in (gqT, gkT, gk, gv):
            r_.merge_subs()
        A.release(m1)
        return R

    def gla(self, l, R):
        A, s, P, I = self.A, self.s, self.P, self.I
        NT = self.NT
        H2 = NT // 2
        gqT, gkT, gk, gv = R["gqT"], R["gkT"], R["gk"], R["gv"]
        cm = {}
        for n in ("gla_ltf", "gla_uf", "gla_ltb", "gla_ub"):
            cm[n] = A.alloc(n, [128, 128], F32)
            s.dma("sp", cm[n].ap, I[n], writes=[cm[n]])
        for n in ("gla_mf", "gla_mb"):
            cm[n] = A.alloc(n, [128, 512], BF16)
            s.dma("pool", cm[n].ap, I[n], writes=[cm[n]])
        ggain = A.alloc("ggain", [128, 128], F32)
        s.dma("sp", ggain.ap, I["gla_gain"][l:l + 1, :].partition_broadcast(128), writes=[ggain])
        NF = 8
        osum3 = [A.alloc("gos%d" % i, [128, 512], F32) for i in range(NF)]
        grt3 = [A.alloc("ggr%d" % i, [128, 512], BF16) for i in range(NF)]
        ss3 = [A.alloc("gss%d" % i, [128, 32], F32) for i in range(NF)]
        tmp = [A.alloc("gtmp%d" % i, [128, 512], F32) for i in range(4)]
        got = [A.alloc("ggo%d" % i, [128, 512], BF16) for i in range(4)]
        junk4 = [A.alloc("gjunk%d" % i, [128, 128], BF16) for i in range(4)]
        ssq3 = [[x_.sub("c%d" % i) for i in range(4)] for x_ in ss3]
        tmq = [[x_.sub("h%d" % i) for i in range(4)] for x_ in tmp]
        Abk = [P[1], P[6]]
        v3 = lambda ap: ap.rearrange("p (a b) -> p a b", a=2)
        dirs = []
        for di, dirn in enumerate(("f", "b")):
            d = dict(dirn=dirn)
            d["LT"] = cm["gla_ltf"] if dirn == "f" else cm["gla_ltb"]
            d["UU"] = cm["gla_uf"] if dirn == "f" else cm["gla_ub"]
            d["MK"] = cm["gla_mf"] if dirn == "f" else cm["gla_mb"]
            d["so"] = 0 if dirn == "f" else 256
            d["lastcol"] = 127 if dirn == "f" else 0
            d["order"] = list(range(NT)) if dirn == "f" else list(range(NT - 1, -1, -1))
            d["S32"] = A.alloc("S32" + dirn, [128, 256], F32)
            d["S32q"] = [[d["S32"].sub("q%d%d" % (pr, a)) for a in range(2)] for pr in range(2)]
            d["Sb"] = A.alloc("Sb" + dirn, [128, 256], BF16)
            for nm, shp, dt in (("E", [128, 512], F32), ("ENB", [128, 256], F32), ("qe", [128, 256], BF16),
                                ("ke", [128, 256], BF16), ("kl", [128, 256], BF16), ("AM", [128, 512], BF16),
                                ("of", [128, 512], F32)):
                d[nm] = [A.alloc("g%s%s%d" % (nm, dirn, i), shp, dt) for i in range(2)]
            d["sp"] = [A.alloc("sptl%s%d" % (dirn, i), [128, 256], F32) for i in range(4)]
            d["O"] = P[2 + di]
            d["D"] = P[4 + di]
            d["BL"] = [P[0], P[7]][di]
            s.op("dve", lambda e, d=d: e.memset(d["S32"].ap, 0.0), writes=[q for r in d["S32q"] for q in r])
            d["AMh"] = [[am.sub("a0"), am.sub("a1")] for am in d["AM"]]
            s.op("dve", lambda e, d=d: e.memset(d["Sb"].ap, 0.0), writes=[d["Sb"]])
            dirs.append(d)

        def load_sp(d, idx):
            t = d["order"][idx]
            sp = d["sp"][idx % 4]
            s.dma("sp", sp.ap, self.spd.rows(t)[:, d["so"]:d["so"] + 256], reads=[self.spd.t[t]], writes=[sp])

        def stageA(d, idx, part):
            t = d["order"][idx]
            k = idx % 2
            sp = d["sp"][idx % 4]
            spt = sp.ap
            BL = d["BL"]
            E, ENB, qe, ke, kl, AM = d["E"][k], d["ENB"][k], d["qe"][k], d["ke"][k], d["kl"][k], d["AM"][k]
            LT, UU, MK = d["LT"], d["UU"], d["MK"]
            if part == 2:
                for hh in range(4):
                    pr, a = hh // 2, hh % 2
                    pl = slice(64 * a, 64 * a + 64)
                    s.op("pe", lambda e, pr=pr, pl=pl, a=a: e.matmul(
                        Abk[a][:, pr * 128:(pr + 1) * 128], lhsT=ke[pl, pr * 128:(pr + 1) * 128],
                        rhs=qe[pl, pr * 128:(pr + 1) * 128], start=True, stop=True),
                         reads=[ke, qe], writes=[Abk[a]], inc=(hh >= 2))
                am4 = AM.ap.rearrange("p (pr a t) -> p pr a t", pr=2, a=2)
                for a in range(2):
                    s.op("dve", lambda e, a=a: e.tensor_tensor(out=am4[:, :, a, :], in0=v3(Abk[a][:, 0:256]),
                                                              in1=v3(MK[:, 0:256]), op=ALU.mult),
                         reads=[Abk[a], MK], writes=[d["AMh"][k][a]])
                if idx > H2:
                    s.dma("sp", d["of"][k].ap, self.ofw.rows(t), reads=[self.ofw.t[t]], writes=[d["of"][k]])
                return
            for pr in range(2):
                s.op("pe", lambda e, pr=pr: e.matmul(BL[:, pr * 128:(pr + 1) * 128], lhsT=spt[:, pr * 128:(pr + 1) * 128],
                                                    rhs=LT.ap, start=True, stop=True),
                     reads=[sp, LT], writes=[BL], inc=False)
            s.op("pe", lambda e: e.matmul(BL[:, 256:512], lhsT=UU.ap, rhs=spt, start=True, stop=True),
                 reads=[sp, UU], writes=[BL])
            s.op("act", lambda e: e.activation(out=E.ap, in_=BL.ap, func=AF.Exp), reads=[BL], writes=[E])
            s.op("act", lambda e: e.activation(out=ENB.ap, in_=BL[:, 0:256], func=AF.Exp, scale=-1.0),
                 reads=[BL], writes=[ENB])
            s.op("dve", lambda e: e.tensor_tensor(out=v3(qe.ap), in0=gqT[:, :, t * 128:(t + 1) * 128],
                                                  in1=v3(E[:, 0:256]), op=ALU.mult),
                 reads=[gqT, E], writes=[qe])
            s.op("pool", lambda e: e.tensor_tensor(out=v3(ke.ap), in0=gkT[:, :, t * 128:(t + 1) * 128],
                                                   in1=v3(ENB.ap), op=ALU.mult),
                 reads=[gkT, ENB], writes=[ke])
            s.op("pool", lambda e: e.tensor_tensor(out=kl.ap, in0=gk[:, t, :], in1=E[:, 256:512], op=ALU.mult),
                 reads=[gk, E], writes=[kl])

        fin_items = []

        def stageB(d, idx):
            t = d["order"][idx]
            k = idx % 2
            E, qe, kl, AM, of = d["E"][k], d["qe"][k], d["kl"][k], d["AM"][k], d["of"][k]
            S32, Sb, D_, O_ = d["S32"], d["Sb"], d["D"], d["O"]
            if idx == H2:
                s.dma("sp", of.ap, self.ofw.rows(t), reads=[self.ofw.t[t]], writes=[of])
            for pr in range(2):
                s.op("pe", lambda e, pr=pr: e.matmul(D_[:, pr * 256:(pr + 1) * 256], lhsT=kl[:, pr * 128:(pr + 1) * 128],
                                                    rhs=gv[:, t, pr * 256:(pr + 1) * 256], start=True, stop=True),
                     reads=[kl, gv], writes=[D_], inc=(pr == 1))
            for hh in range(4):
                pr, a = hh // 2, hh % 2
                pl = slice(64 * a, 64 * a + 64)
                s.op("pe", lambda e, hh=hh: e.matmul(O_[:, hh * 128:(hh + 1) * 128], lhsT=AM[:, hh * 128:(hh + 1) * 128],
                                                    rhs=gv[:, t, hh * 128:(hh + 1) * 128], start=True, stop=False),
                     reads=[d["AMh"][k][hh % 2], gv], writes=[O_], inc=False)
                s.op("pe", lambda e, hh=hh, pr=pr, pl=pl: e.matmul(
                    O_[:, hh * 128:(hh + 1) * 128], lhsT=qe[pl, pr * 128:(pr + 1) * 128],
                    rhs=Sb[pl, pr * 128:(pr + 1) * 128], start=False, stop=True),
                     reads=[qe, Sb], writes=[O_], inc=(hh == 3))
            for pr in range(2):
                for a in range(2):
                    pl = slice(64 * a, 64 * a + 64)
                    col = pr * 128 + d["lastcol"]
                    s.op("dve", lambda e, pr=pr, a=a, pl=pl, col=col: e.scalar_tensor_tensor(
                        out=S32[pl, pr * 128:(pr + 1) * 128], in0=S32[pl, pr * 128:(pr + 1) * 128],
                        scalar=E[pl, col:col + 1], in1=D_[pl, pr * 256 + a * 128:pr * 256 + a * 128 + 128],
                        op0=ALU.mult, op1=ALU.add), reads=[d["S32q"][pr][a], E, D_], writes=[d["S32q"][pr][a]])
            s.op("act", lambda e: e.copy(out=Sb.ap, in_=S32.ap), reads=[q for r in d["S32q"] for q in r], writes=[Sb])
            if idx < H2:
                s.op("act", lambda e: e.copy(out=of.ap, in_=O_.ap), reads=[O_], writes=[of])
                s.dma("sp", self.ofw.rows(t), of.ap, reads=[of], writes=[self.ofw.t[t]])
            else:
                fi = len(fin_items)
                fin_items.append(t)
                os_ = osum3[fi % NF]
                s.op("dve", lambda e: e.tensor_tensor(out=os_.ap, in0=O_.ap, in1=of.ap, op=ALU.add),
                     reads=[O_, of], writes=[os_])

        def fin1(fi):
            t = fin_items[fi]
            os_, ss, gr = osum3[fi % NF], ss3[fi % NF], grt3[fi % NF]
            s.dma("sp", gr.ap, self.grs.rows(t), reads=[self.grs.t[t]], writes=[gr])
            ssq = ssq3[fi % NF]
            s.op("dve", lambda e: e.memset(ss[:, 0:4], 0.0), writes=[ss] + ssq)
            for hh in range(4):
                s.op("act", lambda e, hh=hh: e.activation(out=junk4[hh].ap, in_=os_[:, hh * 128:(hh + 1) * 128],
                                                         func=AF.Square, accum_out=ss[:, hh:hh + 1]),
                     reads=[os_], writes=[junk4[hh], ssq[hh]])
            s.op("dve", lambda e: e.tensor_scalar(out=ss[:, 8:12], in0=ss[:, 0:4], scalar1=1.0 / 128, scalar2=EPS,
                                                  op0=ALU.mult, op1=ALU.add), reads=[ss] + ssq, writes=[ss])

        def fin1b(fi):
            ss = ss3[fi % NF]
            s.op("pool", lambda e: e.tensor_tensor(out=ss[:, 16:20], in0=ss[:, 8:12], in1=self.mhalf[:, 0:4],
                                                   op=ALU.pow), reads=[ss, self.mhalf], writes=[ss])

        def fin2(fi):
            t = fin_items[fi]
            os_, ss, gr = osum3[fi % NF], ss3[fi % NF], grt3[fi % NF]
            tm, go = tmp[fi % 4], got[fi % 4]
            for hh in range(4):
                s.op("dve", lambda e, hh=hh: e.scalar_tensor_tensor(
                    out=tm[:, hh * 128:(hh + 1) * 128], in0=os_[:, hh * 128:(hh + 1) * 128],
                    scalar=ss[:, 16 + hh:17 + hh], in1=ggain.ap, op0=ALU.mult, op1=ALU.mult),
                     reads=[os_, ss, ggain], writes=[tmq[fi % 4][hh]])
            s.op("pool", lambda e: e.tensor_tensor(out=go.ap, in0=tm.ap, in1=gr.ap, op=ALU.mult),
                 reads=tmq[fi % 4] + [gr], writes=[go])
            s.dma("sp", self.glo.rows(t), go.ap, reads=[go], writes=[self.glo.t[t]])

        for d in dirs:
            load_sp(d, 0)
            if NT > 1:
                load_sp(d, 1)
        for part in (1, 2):
            for d in dirs:
                stageA(d, 0, part)
        n1 = n2 = nb = 0
        for idx in range(NT):
            for d in dirs:
                if idx + 2 < NT:
                    load_sp(d, idx + 2)
            if idx + 1 < NT:
                for part in (1, 2):
                    for d in dirs:
                        stageA(d, idx + 1, part)
            nf_before = len(fin_items)
            for d in dirs:
                stageB(d, idx)
            while n2 < nb:
                fin2(n2)
                n2 += 1
            while nb < n1:
                fin1b(nb)
                nb += 1
            while n1 < nf_before:
                fin1(n1)
                n1 += 1
        while n1 < len(fin_items):
            fin1(n1)
            n1 += 1
        while nb < len(fin_items):
            fin1b(nb)
            nb += 1
        while n2 < len(fin_items):
            fin2(n2)
            n2 += 1
        self.tile_order = list(fin_items)

    def wout(self, l, pre=None):
        A, s, P, I = self.A, self.s, self.P, self.I
        m0 = A.mark()
        W = A.alloc("wout", [128, 8, D], BF16)
        s.dma("pool", W.ap, I["w_out"][l].rearrange("(c p) d -> p c d", p=128), writes=[W])
        if pre is not None:
            self.ffn_load_gu(pre)
        gain = self.load_gain("na_gain", l, 512)
        nat = [A.alloc("wnat%d" % i, [128, 512], F32) for i in range(2)]
        mm = [A.alloc("wm%d" % i, [128, D], BF16) for i in range(2)]
        xt = [A.alloc("wx%d" % i, [128, D], F32) for i in range(3)]
        mT = [A.alloc("wmT%d" % i, [128, 8, 128], BF16) for i in range(2)]
        junk = A.alloc("wjunk", [128, 512], BF16)
        ss = A.alloc("wss", [128, 32], F32)
        Tb = [P[0], P[1]]
        Yb = [P[2], P[3], P[4], P[5]]
        seqpos = {}

        def front(t):
            q = seqpos.setdefault(t, len(seqpos))
            na_, m, x = nat[q % 2], mm[q % 2], xt[q % 3]
            s.dma("sp", na_.ap, self.nao.rows(t), reads=[self.nao.t[t]], writes=[na_])
            s.dma("sp", m[:, 512:1024], self.glo.rows(t), reads=[self.glo.t[t]], writes=[m])
            s.dma("sp", x.ap, self.xs.rows(t), reads=[self.xs.t[t]], writes=[x])
            s.op("dve", lambda e: e.memset(ss[:, 0:1], 0.0), writes=[ss])
            s.op("act", lambda e: e.activation(out=junk.ap, in_=na_.ap, func=AF.Square, accum_out=ss[:, 0:1]),
                 reads=[na_], writes=[junk, ss])
            s.op("dve", lambda e: e.tensor_scalar(out=ss[:, 8:9], in0=ss[:, 0:1], scalar1=1.0 / 512, scalar2=EPS,
                                                  op0=ALU.mult, op1=ALU.add), reads=[ss], writes=[ss])
            s.op("pool", lambda e: e.tensor_tensor(out=ss[:, 16:17], in0=ss[:, 8:9], in1=self.mhalf[:, 0:1], op=ALU.pow),
                 reads=[ss, self.mhalf], writes=[ss])
            s.op("dve", lambda e: e.scalar_tensor_tensor(out=m[:, 0:512], in0=na_.ap, scalar=ss[:, 16:17],
                                                         in1=gain.ap, op0=ALU.mult, op1=ALU.mult),
                 reads=[na_, ss, gain], writes=[m])

        def back(t):
            q = seqpos[t]
            m, x, mt = mm[q % 2], xt[q % 3], mT[q % 2]
            self.transpose_to(m, Tb[q % 2], mt, mt.ap)
            for hf in range(2):
                Y = Yb[(2 * q + hf) % 4]
                for c in range(8):
                    s.op("pe", lambda e, c=c, hf=hf, Y=Y: e.matmul(Y.ap, lhsT=mt[:, c, :],
                                                                  rhs=W[:, c, hf * 512:(hf + 1) * 512],
                                                                  start=(c == 0), stop=(c == 7)),
                         reads=[mt, W], writes=[Y], inc=(c == 7))
                s.op("dve", lambda e, hf=hf, Y=Y: e.tensor_tensor(out=x[:, hf * 512:(hf + 1) * 512], in0=Y.ap,
                                                                 in1=x[:, hf * 512:(hf + 1) * 512], op=ALU.add),
                     reads=[Y, x], writes=[x])
            s.dma("pool", self.xs.rows(t), x.ap, reads=[x], writes=[self.xs.t[t]])

        order = getattr(self, "tile_order", None) or list(range(self.NT))
        assert sorted(order) == list(range(self.NT))
        front(order[0])
        for i, t in enumerate(order):
            if i + 1 < self.NT:
                front(order[i + 1])
            back(t)
        pos = {t: i for i, t in enumerate(order)}
        self.block_order = sorted(range(self.NB), key=lambda b: max(pos[4 * b + j] for j in range(4)))
        A.release(m0)


_CACHE = {}


def _prog(S, L):
    key = (S, L)
    if key not in _CACHE:
        p = Prog(S, L)
        p.build()
        _CACHE[key] = p
    return _CACHE[key]


PARAMS = ("ffn1_norm", "ffn1_wg", "ffn1_wu", "ffn1_wd", "mix_norm", "w_in", "na_gain", "gla_gain", "w_out",
          "ffn2_norm", "ffn2_wg", "ffn2_wu", "ffn2_wd")


def make_in_maps(inputs, p, ncores):
    L = p.L
    shared = {}
    for n in PARAMS:
        shared[n] = np.ascontiguousarray(np.asarray(inputs[n], np.float32))
    shared["final_norm"] = np.ascontiguousarray(np.asarray(inputs["final_norm"], np.float32).reshape(1, D))
    shared.update(host_layout(inputs, L))
    shared.update(p.consts)
    x = np.asarray(inputs["x"], np.float32)
    maps = []
    for c in range(ncores):
        m = dict(shared)
        m["x"] = np.ascontiguousarray(x[c])
        maps.append(m)
    return maps


def kernel(**inputs):
    x = np.asarray(inputs["x"])
    B, S, _ = x.shape
    L = np.asarray(inputs["ffn1_norm"]).shape[0]
    p = _prog(S, L)
    maps = make_in_maps(inputs, p, B)
    res = run_bass_kernel_spmd(p.nc, maps, core_ids=list(range(B)))
    return np.stack([np.asarray(r["out"], np.float32) for r in res.results], axis=0)
```

```python
import numpy as np
from contextlib import ExitStack
import concourse.bass as bass
import concourse.mybir as mybir
from concourse.bass_utils import run_bass_kernel_spmd

F32 = mybir.dt.float32
BF16 = mybir.dt.bfloat16
U8 = mybir.dt.uint8
AF = mybir.ActivationFunctionType
ALU = mybir.AluOpType

D = 1024
DFF = 2816
NFC = DFF // 128
NA_H = 8
GL_H = 4
INW = 3104
EPS = 1e-6
NEG = -30000.0
NDMA_SLOTS = 8


class Dep:
    __slots__ = ("key", "sem", "val", "owner")

    def __init__(self, key, sem, val, owner):
        self.key, self.sem, self.val, self.owner = key, sem, val, owner


class Res:
    def __init__(self, name="", ap=None):
        self.name = name
        self.ap = ap
        self.w = None
        self.r = {}
        self.children = []
        self.extra_w = []

    def sub(self, name):
        c = Res(name, self.ap)
        c.r = dict(self.r)
        c.w = self.w
        self.children.append(c)
        return c

    def merge_subs(self):
        best = {}
        for c in self.children:
            if c.w is not None and (c.w.key not in best or best[c.w.key].val < c.w.val):
                best[c.w.key] = c.w
        self.extra_w = list(best.values())

    def all_deps(self):
        ds = list(self.r.values()) + ([self.w] if self.w is not None else [])
        for c in self.children:
            ds += c.all_deps()
        return ds

    def __getitem__(self, k):
        return self.ap[k]


class Eng:
    def __init__(self, name, h, sem, key):
        self.name, self.h, self.sem, self.key = name, h, sem, key
        self.cnt = 0
        self.seen = {}
        self.slots = []
        self.ndma = 0


class Sched:
    def __init__(self, nc, es):
        self.nc = nc
        self.nkey = 0
        self.E = {}
        for name, h in (("pe", nc.tensor), ("act", nc.scalar), ("dve", nc.vector),
                        ("pool", nc.gpsimd), ("sp", nc.sync)):
            sem = es.enter_context(nc.semaphore("s_" + name))
            self.E[name] = Eng(name, h, sem, self._k())
        for name in ("sp", "pool"):
            e = self.E[name]
            for i in range(NDMA_SLOTS):
                e.slots.append((self._k(), es.enter_context(nc.semaphore("d_%s%d" % (name, i)))))

    def _k(self):
        self.nkey += 1
        return self.nkey

    def _wait(self, eng, d):
        if eng.seen.get(d.key, 0) >= d.val:
            return
        eng.h.wait_ge(d.sem, d.val)
        eng.seen[d.key] = d.val

    def _deps(self, en, reads, writes):
        eng = self.E[en]
        deps = {}

        def add(d):
            o = deps.get(d.key)
            if o is None or o.val < d.val:
                deps[d.key] = d

        for r in reads:
            if r.w is not None:
                add(r.w)
            for d in r.extra_w:
                add(d)
        for w in writes:
            if w.w is not None:
                add(w.w)
            for d in w.r.values():
                add(d)
        for d in deps.values():
            if d.owner == "pe" and en == "pe":
                continue
            self._wait(eng, d)

    def _commit(self, dep, reads, writes):
        for w in writes:
            w.w = dep
            w.r = {}
        for r in reads:
            if any(r is w for w in writes):
                continue
            o = r.r.get(dep.key)
            if o is None or o.val < dep.val:
                r.r[dep.key] = dep

    def op(self, en, fn, reads=(), writes=(), inc=True):
        eng = self.E[en]
        self._deps(en, reads, writes)
        ins = fn(eng.h)
        if inc:
            eng.cnt += 1
            ins.then_inc(eng.sem, 1)
            dep = Dep(eng.key, eng.sem, eng.cnt, en)
        else:
            dep = Dep(eng.key, eng.sem, eng.cnt + 1, en)
        self._commit(dep, reads, writes)
        return dep

    def dma(self, en, out, in_, reads=(), writes=()):
        eng = self.E[en]
        i = eng.ndma
        eng.ndma += 1
        key, sem = eng.slots[i % NDMA_SLOTS]
        rnd = i // NDMA_SLOTS
        self._deps(en, reads, writes)
        if rnd > 0:
            self._wait(eng, Dep(key, sem, 16 * rnd, "dma"))
        eng.h.dma_start(out=out, in_=in_).then_inc(sem, 16)
        dep = Dep(key, sem, 16 * (rnd + 1), "dma")
        self._commit(dep, reads, writes)
        return dep

    def finish(self):
        sp = self.E["sp"]
        for e in self.E.values():
            for j, (key, sem) in enumerate(e.slots):
                n = (e.ndma - j + NDMA_SLOTS - 1) // NDMA_SLOTS
                if n > 0:
                    self._wait(sp, Dep(key, sem, 16 * n, "dma"))
        for e in self.E.values():
            if e.name != "sp" and e.cnt > 0:
                self._wait(sp, Dep(e.key, e.sem, e.cnt, e.name))


def _esz(dt):
    return 2 if dt == BF16 else 4


class Arena:
    def __init__(self, nc, nbytes):
        self.t = nc.alloc_sbuf_tensor("arena", [128, nbytes], U8)
        self.size = nbytes
        self.top = 0
        self.live = []
        self.ghosts = []

    def mark(self):
        return (self.top, len(self.live))

    def release(self, m):
        top, n = m
        while len(self.live) > n:
            self.ghosts.append(self.live.pop())
        self.top = top

    def alloc(self, name, shape, dt):
        assert shape[0] == 128
        n = int(np.prod(shape[1:])) * _esz(dt)
        start = (self.top + 31) // 32 * 32
        end = start + n
        assert end <= self.size, "SBUF arena overflow at %s: need %d have %d" % (name, end, self.size)
        self.top = end
        ap = self.t[:, start:end].bitcast(dt)
        if len(shape) == 3:
            ap = ap.rearrange("p (a b) -> p a b", a=shape[1], b=shape[2])
        elif len(shape) == 4:
            ap = ap.rearrange("p (a b c) -> p a b c", a=shape[1], b=shape[2], c=shape[3])
        r = Res(name, ap)
        keep = []
        for (s, e, g) in self.ghosts:
            if s < end and e > start:
                for d in g.all_deps():
                    o = r.r.get(d.key)
                    if o is None or o.val < d.val:
                        r.r[d.key] = d
                if s >= start and e <= end:
                    continue
            keep.append((s, e, g))
        self.ghosts = keep
        self.live.append((start, end, r))
        return r


class DT:
    def __init__(self, nc, name, shape, dt, kind="Internal", ntile=None):
        self.h = nc.dram_tensor(name, list(shape), dt, kind=kind)
        self.ap = self.h.ap()
        nt = ntile if ntile is not None else max(1, shape[0] // 128)
        self.t = [Res("%s[%d]" % (name, i)) for i in range(nt)]

    def rows(self, i, n=1):
        return self.ap[i * 128:(i + n) * 128]


def na_geometry(S):
    R = S // 64
    wh = min(8, R)
    G = R // 4
    rs = lambda r: int(np.clip(r - wh // 2, 0, R - wh))
    cs = lambda c: int(np.clip(c - 8, 0, 64 - 16))
    colvalid = np.zeros((64, 64), bool)
    for c in range(64):
        colvalid[cs(c):cs(c) + 16, c] = True
    types = []
    tindex = {}
    groups = []
    for g in range(G):
        lo = rs(4 * g)
        hi = rs(4 * g + 3) + wh
        kb = lo - (lo % 2)
        ke = hi + (hi % 2)
        chunks = []
        for j in range((ke - kb) // 2):
            m = np.full((2, 64, 4, 64), NEG, np.float32)
            rel = [False, False]
            for b in range(2):
                kr = kb + 2 * j + b
                for a in range(4):
                    r = 4 * g + a
                    if rs(r) <= kr < rs(r) + wh:
                        m[b, :, a, :] = np.where(colvalid, 0.0, NEG)
                        rel[a // 2] = True
            e0 = 7 - (kb + 2 * j - 4 * g)
            key = (m.tobytes(), e0)
            if key not in tindex:
                tindex[key] = len(types)
                types.append((m.reshape(128, 256), e0))
            chunks.append(dict(kt=(kb + 2 * j) // 2, ty=tindex[key], rel=rel))
        groups.append(chunks)
    return groups, types


def host_consts(S):
    groups, types = na_geometry(S)
    c = {}
    c["ident"] = np.eye(128, dtype=np.float32)
    s = np.arange(128)
    g = -1.0 / 16.0
    c["gla_ltf"] = np.where(s[:, None] <= s[None, :], g, 0.0).astype(np.float32)
    c["gla_uf"] = np.where(s[:, None] > s[None, :], g, 0.0).astype(np.float32)
    c["gla_ltb"] = np.where(s[:, None] >= s[None, :], g, 0.0).astype(np.float32)
    c["gla_ub"] = np.where(s[:, None] < s[None, :], g, 0.0).astype(np.float32)
    mf = (s[:, None] <= s[None, :]).astype(np.float32)
    mb = (s[:, None] > s[None, :]).astype(np.float32)
    c["gla_mf"] = np.tile(mf, (1, 4))
    c["gla_mb"] = np.tile(mb, (1, 4))
    c["na_mask"] = np.stack([t[0] for t in types]).astype(np.float32)
    return c


def host_layout(inp, L):
    o = {}
    rpb = np.asarray(inp["na_rpb"], np.float32)
    cp = np.arange(64)[:, None]
    cc = np.arange(64)[None, :]
    dc = np.clip(cp - cc + 15, 0, 30)
    tr = np.zeros((L, NA_H, 64, 31, 64), np.float32)
    for e in range(15):
        tr[:, :, :, e + 8, :] = rpb[:, :, 14 - e, :][:, :, dc]
    o["na_tr"] = tr
    wg = np.zeros((L, 2, 17, 256), np.float32)
    wg[:, 0, :16] = inp["w_gate_f"]
    wg[:, 0, 16] = inp["b_gate_f"]
    wg[:, 1, :16] = inp["w_gate_b"]
    wg[:, 1, 16] = inp["b_gate_b"]
    o["wgate"] = wg
    return o


class Prog:
    def __init__(self, S, L, stop_after=None, dbg=()):
        self.S, self.L = S, L
        self.NT = S // 128
        self.NB = S // 512
        self.stop_after = stop_after
        self.dbg = set(dbg)
        nc = self.nc = bass.Bass("TRN2", target_bir_lowering=False)
        self.consts = host_consts(S)
        self.groups, self.types = na_geometry(S)
        ein = lambda n, sh: nc.dram_tensor(n, list(sh), F32, kind="ExternalInput").ap()
        I = self.I = {}
        I["x"] = DT(nc, "x", [S, D], F32, kind="ExternalInput")
        for n, sh in (("ffn1_norm", [L, D]), ("ffn1_wg", [L, D, DFF]), ("ffn1_wu", [L, D, DFF]),
                      ("ffn1_wd", [L, DFF, D]), ("mix_norm", [L, D]), ("w_in", [L, D, INW]),
                      ("na_gain", [L, 512]), ("gla_gain", [L, 128]), ("w_out", [L, D, D]),
                      ("ffn2_norm", [L, D]), ("ffn2_wg", [L, D, DFF]), ("ffn2_wu", [L, D, DFF]),
                      ("ffn2_wd", [L, DFF, D]), ("final_norm", [1, D]),
                      ("na_tr", [L, NA_H, 64, 31, 64]), ("wgate", [L, 2, 17, 256])):
            I[n] = ein(n, sh)
        for n, v in self.consts.items():
            I[n] = ein(n, v.shape)
        self.out = DT(nc, "out", [S, D], F32, kind="ExternalOutput")
        self.xs = DT(nc, "xs", [S, D], F32, kind=self._kind("xs"))
        self.nao = DT(nc, "nao", [S, 512], F32, kind=self._kind("nao"))
        self.glo = DT(nc, "glo", [S, 512], BF16, kind=self._kind("glo"))
        self.ofw = DT(nc, "ofw", [S, 512], F32, kind=self._kind("ofw"))
        self.grs = DT(nc, "grs", [S, 512], BF16, kind=self._kind("grs"))
        self.spd = DT(nc, "spd", [S, 512], F32, kind=self._kind("spd"))

    def _kind(self, n):
        return "ExternalOutput" if n in self.dbg else "Internal"

    def dump(self, name, res, ap=None):
        if name not in self.dbg:
            return
        ap = res.ap if ap is None else ap
        d = self.nc.dram_tensor("dbg_" + name, list(ap.shape), ap.dtype, kind="ExternalOutput").ap()
        self.s.dma("sp", d, ap, reads=[res])

    def bcast_row(self, ap_row, n):
        return ap_row.partition_broadcast(128)

    def build(self):
        nc = self.nc
        with ExitStack() as es:
            es.enter_context(nc.allow_low_precision("bf16 matmul operands, fp32 accumulation"))
            es.enter_context(nc.allow_non_contiguous_dma(reason="tiled layouts"))
            self.s = Sched(nc, es)
            self.A = Arena(nc, 212480)
            self.P = []
            for i in range(8):
                t = nc.alloc_psum_tensor("bank%d" % i, [128, 512], F32)
                self.P.append(Res("bank%d" % i, t[:, :]))
            A, s = self.A, self.s
            self.ident = A.alloc("ident", [128, 128], BF16)
            s.dma("pool", self.ident.ap, self.I["ident"], writes=[self.ident])
            self.mhalf = A.alloc("mhalf", [128, 8], F32)
            s.op("pool", lambda e: e.memset(self.mhalf.ap, -0.5), writes=[self.mhalf])
            self.body()
            s.finish()
        return nc

    def body(self):
        src = self.I["x"]
        for l in range(self.L):
            self.ffn(l, "ffn1", src, self.xs)
            src = self.xs
            if self.stop_after == ("ffn1", l):
                break
            mb = self.A.mark()
            pre = self.mixer(l)
            if self.stop_after is not None and self.stop_after[1] == l and \
                    self.stop_after[0] in ("mix", "proj_na", "na", "gla", "proj_gla"):
                self.A.release(mb)
                break
            self.ffn(l, "ffn2", self.xs, self.xs, pre=pre)
            self.A.release(mb)
            if self.stop_after == ("ffn2", l):
                break
        self.final_norm(src)

    def load_gain(self, name, l, n):
        A, s = self.A, self.s
        g = A.alloc("gain_" + name, [128, n], F32)
        s.dma("sp", g.ap, self.I[name][l:l + 1, :].partition_broadcast(128), writes=[g])
        return g

    def norm_tiles(self, xts, gain, hbs, ss, junk, n=D):
        s = self.s
        k = len(xts)
        s.op("dve", lambda e: e.memset(ss[:, 0:8], 0.0), writes=[ss])
        for j, (xr, xa) in enumerate(xts):
            s.op("act", lambda e, xa=xa, j=j: e.activation(out=junk[:, 0:n], in_=xa, func=AF.Square,
                                                          accum_out=ss[:, j:j + 1]),
                 reads=[xr], writes=[junk, ss])
        s.op("dve", lambda e: e.tensor_scalar(out=ss[:, 8:8 + k], in0=ss[:, 0:k], scalar1=1.0 / n, scalar2=EPS,
                                              op0=ALU.mult, op1=ALU.add), reads=[ss], writes=[ss])
        s.op("act", lambda e: e.activation(out=ss[:, 8:8 + k], in_=ss[:, 8:8 + k], func=AF.Sqrt),
             reads=[ss], writes=[ss])
        s.op("dve", lambda e: e.reciprocal(out=ss[:, 16:16 + k], in_=ss[:, 8:8 + k]), reads=[ss], writes=[ss])
        for j, (xr, xa) in enumerate(xts):
            hb = hbs[j]
            s.op("dve", lambda e, xa=xa, j=j, hb=hb: e.scalar_tensor_tensor(
                out=hb[:, 0:n], in0=xa, scalar=ss[:, 16 + j:17 + j], in1=gain[:, 0:n],
                op0=ALU.mult, op1=ALU.mult), reads=[xr, ss, gain], writes=[hb])

    def transpose_to(self, hb, bank, dst, dst_ap, nchunk=8):
        s = self.s
        pb = bank.ap.bitcast(BF16)
        for c in range(nchunk):
            s.op("pe", lambda e, c=c: e.transpose(out=pb[:, c * 128:(c + 1) * 128], in_=hb[:, c * 128:(c + 1) * 128],
                                                 identity=self.ident.ap),
                 reads=[hb, self.ident], writes=[bank], inc=(c == nchunk - 1))
        s.op("act", lambda e: e.copy(out=dst_ap, in_=pb[:, 0:nchunk * 128].rearrange("p (c t) -> p c t", c=nchunk)),
             reads=[bank], writes=[dst])

    def ffn(self, l, which, src, dst, pre=None):
        A, s, I, P = self.A, self.s, self.I, self.P
        NB = self.NB
        m0 = A.mark()
        if pre is None:
            pre = self.ffn_alloc_gu(l, which)
        wg, wu, wg_q, wu_q = pre["wg"], pre["wu"], pre["wg_q"], pre["wu_q"]
        wd = A.alloc("wd", [128, NFC, D], BF16)
        wd_q = [wd.sub("wdq%d" % q) for q in range(2)]
        FQ = DFF // 4
        dsrc = I[which + "_wd"][l].rearrange("(c p) d -> p c d", p=128)
        gain = self.load_gain(which + "_norm", l, D)
        if not pre["loaded"]:
            self.ffn_load_gu(pre)
        for q in range(2):
            s.dma("pool", wd[:, q * 11:(q + 1) * 11, :], dsrc[:, q * 11:(q + 1) * 11, :], writes=[wd_q[q]])
        xn = [A.alloc("xn%d" % i, [128, D], F32) for i in range(4)]
        xr = [A.alloc("xr%d" % i, [128, D], F32) for i in range(2)]
        hb = [A.alloc("hb%d" % i, [128, D], BF16) for i in range(4)]
        hT = [A.alloc("hT%d" % i, [128, 8, 512], BF16) for i in range(2)]
        aT = A.alloc("aT", [128, NFC, 512], BF16)
        sg = [A.alloc("sg0", [128, 512], BF16)] * 2
        ss = A.alloc("ss", [128, 32], F32)
        Tb = [P[0], P[1]]
        Gb = [P[2], P[3]]
        Ub = [P[4], P[5]]
        Yb = [P[6], P[7]]

        def norm_act(b, q):
            s.op("dve", lambda e: e.memset(ss[:, 0:8], 0.0), writes=[ss])
            for j in range(4):
                t = b * 4 + j
                s.dma("sp", xn[j].ap, src.rows(t), reads=[src.t[t]], writes=[xn[j]])
                s.op("act", lambda e, j=j: e.activation(out=hb[j].ap, in_=xn[j].ap, func=AF.Square,
                                                       accum_out=ss[:, j:j + 1]),
                     reads=[xn[j]], writes=[hb[j], ss])

        def norm_dve(b, q):
            s.op("dve", lambda e: e.tensor_scalar(out=ss[:, 8:12], in0=ss[:, 0:4], scalar1=1.0 / D, scalar2=EPS,
                                                  op0=ALU.mult, op1=ALU.add), reads=[ss], writes=[ss])
            s.op("pool", lambda e: e.tensor_tensor(out=ss[:, 16:20], in0=ss[:, 8:12], in1=self.mhalf[:, 0:4], op=ALU.pow),
                 reads=[ss, self.mhalf], writes=[ss])
            for j in range(4):
                s.op("dve", lambda e, j=j: e.scalar_tensor_tensor(
                    out=hb[j].ap, in0=xn[j].ap, scalar=ss[:, 16 + j:17 + j], in1=gain.ap,
                    op0=ALU.mult, op1=ALU.mult), reads=[xn[j], ss, gain], writes=[hb[j]])

        def norm_T(b, q):
            for j in range(4):
                self.transpose_to(hb[j], Tb[j % 2], hT[q % 2], hT[q % 2][:, :, j * 128:(j + 1) * 128])

        def stage_gu(b, q):
            h = hT[q % 2]
            for fc in range(NFC):
                q = fc * 128 // FQ
                q2 = (fc * 128 + 127) // FQ
                gq = [wg_q[q]] + ([wg_q[q2]] if q2 != q else [])
                uq = [wu_q[q]] + ([wu_q[q2]] if q2 != q else [])
                G, U = Gb[fc % 2], Ub[fc % 2]
                for c in range(8):
                    s.op("pe", lambda e, c=c, fc=fc, G=G: e.matmul(G.ap, lhsT=wg[:, c, fc * 128:(fc + 1) * 128],
                                                                  rhs=h[:, c, :], start=(c == 0), stop=(c == 7)),
                         reads=[h] + gq, writes=[G], inc=(c == 7))
                for c in range(8):
                    s.op("pe", lambda e, c=c, fc=fc, U=U: e.matmul(U.ap, lhsT=wu[:, c, fc * 128:(fc + 1) * 128],
                                                                  rhs=h[:, c, :], start=(c == 0), stop=(c == 7)),
                         reads=[h] + uq, writes=[U], inc=(c == 7))
                sgb = sg[fc % 2]
                s.op("act", lambda e, G=G, sgb=sgb: e.activation(out=sgb.ap, in_=G.ap, func=AF.Silu),
                     reads=[G], writes=[sgb])
                s.op("dve", lambda e, U=U, sgb=sgb, fc=fc: e.tensor_tensor(out=aT[:, fc, :], in0=U.ap, in1=sgb.ap,
                                                                          op=ALU.mult),
                     reads=[U, sgb], writes=[aT])

        def stage_down(b, hooks):
            for j in range(4):
                if j in hooks:
                    hooks[j]()
                t = b * 4 + j
                xt = xr[j % 2]
                s.dma("sp", xt.ap, src.rows(t), reads=[src.t[t]], writes=[xt])
                for hf in range(2):
                    Y = Yb[hf]
                    for fc in range(NFC):
                        s.op("pe", lambda e, fc=fc, hf=hf, j=j, Y=Y: e.matmul(
                            Y.ap, lhsT=aT[:, fc, j * 128:(j + 1) * 128], rhs=wd[:, fc, hf * 512:(hf + 1) * 512],
                            start=(fc == 0), stop=(fc == NFC - 1)),
                             reads=[aT, wd_q[fc // 11]], writes=[Y], inc=(fc == NFC - 1))
                    s.op("dve", lambda e, hf=hf, Y=Y, xt=xt: e.scalar_tensor_tensor(
                        out=xt[:, hf * 512:(hf + 1) * 512], in0=Y.ap, scalar=0.5, in1=xt[:, hf * 512:(hf + 1) * 512],
                        op0=ALU.mult, op1=ALU.add), reads=[Y, xt], writes=[xt])
                s.dma("pool", dst.rows(t), xt.ap, reads=[xt], writes=[dst.t[t]])

        border = list(range(NB))
        if which == "ffn2" and getattr(self, "block_order", None):
            border = list(self.block_order)
        norm_act(border[0], 0)
        norm_dve(border[0], 0)
        norm_T(border[0], 0)
        for i, b in enumerate(border):
            stage_gu(b, i)
            if i + 1 < NB:
                nb = border[i + 1]
                norm_act(nb, i + 1)
                stage_down(b, {1: lambda nb=nb, i=i: norm_dve(nb, i + 1), 2: lambda nb=nb, i=i: norm_T(nb, i + 1)})
            else:
                stage_down(b, {})
        A.release(m0)

    def final_norm(self, src):
        A, s = self.A, self.s
        m0 = A.mark()
        gain = A.alloc("gain_fin", [128, D], F32)
        s.dma("sp", gain.ap, self.I["final_norm"][0:1, :].partition_broadcast(128), writes=[gain])
        xt = [A.alloc("fx%d" % i, [128, D], F32) for i in range(3)]
        ot = [A.alloc("fo%d" % i, [128, D], F32) for i in range(3)]
        junk = A.alloc("fjunk", [128, D], BF16)
        ss = A.alloc("fss", [128, 32], F32)
        for t in range(self.NT):
            x = xt[t % 3]
            o = ot[t % 3]
            s.dma("sp", x.ap, src.rows(t), reads=[src.t[t]], writes=[x])
            s.op("dve", lambda e: e.memset(ss[:, 0:1], 0.0), writes=[ss])
            s.op("act", lambda e, x=x: e.activation(out=junk.ap, in_=x.ap, func=AF.Square, accum_out=ss[:, 0:1]),
                 reads=[x], writes=[junk, ss])
            s.op("dve", lambda e: e.tensor_scalar(out=ss[:, 8:9], in0=ss[:, 0:1], scalar1=1.0 / D, scalar2=EPS,
                                                  op0=ALU.mult, op1=ALU.add), reads=[ss], writes=[ss])
            s.op("pool", lambda e: e.tensor_tensor(out=ss[:, 16:17], in0=ss[:, 8:9], in1=self.mhalf[:, 0:1], op=ALU.pow),
                 reads=[ss, self.mhalf], writes=[ss])
            s.op("dve", lambda e, x=x, o=o: e.scalar_tensor_tensor(out=o.ap, in0=x.ap, scalar=ss[:, 16:17], in1=gain.ap,
                                                                  op0=ALU.mult, op1=ALU.mult),
                 reads=[x, ss, gain], writes=[o])
            s.dma("pool", self.out.rows(t), o.ap, reads=[o], writes=[self.out.t[t]])
        A.release(m0)

    def mixer(self, l):
        A, s, I = self.A, self.s, self.I
        mb = A.mark()
        m0 = A.mark()
        res = self.proj_na(l)
        if self.stop_after != ("proj_na", l):
            self.na(l, *res)
        A.release(m0)
        if self.stop_after in (("na", l), ("proj_na", l)):
            return None
        WB = A.alloc("winB", [128, 8, 1568], BF16)
        wsrc = I["w_in"][l].rearrange("(c p) f -> p c f", p=128)
        WBq = [WB.sub("winB%d" % i) for i in range(4)]
        for i, (lo, hi) in enumerate([(0, 512), (512, 1024), (1024, 1536), (1536, 1568)]):
            s.dma("pool", WB[:, :, lo:hi], wsrc[:, :, 1536 + lo:1536 + hi], writes=[WBq[i]])
        res = self.proj_gla(l, WB, WBq)
        if self.stop_after != ("proj_gla", l):
            self.gla(l, res)
        A.release(mb)
        if self.stop_after in (("gla", l), ("proj_gla", l)):
            return None
        pre = self.ffn_alloc_gu(l, "ffn2")
        self.wout(l, pre)
        return pre

    def ffn_alloc_gu(self, l, which):
        A = self.A
        wg = A.alloc("wg", [128, 8, DFF], BF16)
        wu = A.alloc("wu", [128, 8, DFF], BF16)
        wg_q = [wg.sub("wgq%d" % q) for q in range(4)]
        wu_q = [wu.sub("wuq%d" % q) for q in range(4)]
        return dict(wg=wg, wu=wu, wg_q=wg_q, wu_q=wu_q, loaded=False, l=l, which=which)

    def ffn_load_gu(self, pre):
        s, I = self.s, self.I
        l, which = pre["l"], pre["which"]
        FQ = DFF // 4
        gsrc = I[which + "_wg"][l].rearrange("(c p) f -> p c f", p=128)
        usrc = I[which + "_wu"][l].rearrange("(c p) f -> p c f", p=128)
        for q in range(4):
            s.dma("pool", pre["wg"][:, :, q * FQ:(q + 1) * FQ], gsrc[:, :, q * FQ:(q + 1) * FQ], writes=[pre["wg_q"][q]])
            s.dma("pool", pre["wu"][:, :, q * FQ:(q + 1) * FQ], usrc[:, :, q * FQ:(q + 1) * FQ], writes=[pre["wu_q"][q]])
        pre["loaded"] = True

    def proj_blocks(self, l, emit):
        A, s, P = self.A, self.s, self.P
        gain = self.load_gain("mix_norm", l, D)
        xn = [A.alloc("pxn%d" % i, [128, D], F32) for i in range(4)]
        hb = [A.alloc("phb%d" % i, [128, D], BF16) for i in range(4)]
        hT = [A.alloc("phT%d" % i, [128, 8, 512], BF16) for i in range(2)]
        ss = A.alloc("pss", [128, 32], F32)
        Tb = [P[0], P[1]]

        ssq = [ss.sub("c%d" % j) for j in range(4)]

        def frontA(b):
            s.op("dve", lambda e: e.memset(ss[:, 0:8], 0.0), writes=[ss] + ssq)
            for j in range(4):
                t = b * 4 + j
                s.dma("sp", xn[j].ap, self.xs.rows(t), reads=[self.xs.t[t]], writes=[xn[j]])
                s.op("act", lambda e, j=j: e.activation(out=hb[j].ap, in_=xn[j].ap, func=AF.Square,
                                                       accum_out=ss[:, j:j + 1]),
                     reads=[xn[j]], writes=[hb[j], ssq[j]])
            s.op("dve", lambda e: e.tensor_scalar(out=ss[:, 8:12], in0=ss[:, 0:4], scalar1=1.0 / D, scalar2=EPS,
                                                  op0=ALU.mult, op1=ALU.add), reads=[ss] + ssq, writes=[ss])
            s.op("pool", lambda e: e.tensor_tensor(out=ss[:, 16:20], in0=ss[:, 8:12], in1=self.mhalf[:, 0:4], op=ALU.pow),
                 reads=[ss, self.mhalf], writes=[ss])
            for j in range(4):
                s.op("dve", lambda e, j=j: e.scalar_tensor_tensor(
                    out=hb[j].ap, in0=xn[j].ap, scalar=ss[:, 16 + j:17 + j], in1=gain.ap,
                    op0=ALU.mult, op1=ALU.mult), reads=[xn[j], ss, gain], writes=[hb[j]])

        def frontB(b):
            h = hT[b % 2]
            for j in range(4):
                self.transpose_to(hb[j], Tb[j % 2], h, h[:, :, j * 128:(j + 1) * 128])

        frontA(0)
        frontB(0)
        for b in range(self.NB):
            if b + 1 < self.NB:
                frontA(b + 1)
                emit(b, hT[b % 2], lambda b=b: frontB(b + 1))
            else:
                emit(b, hT[b % 2], lambda: None)

    def proj_na(self, l, after_w=None):
        A, s, P, I = self.A, self.s, self.P, self.I
        S, NT = self.S, self.NT
        QT = A.alloc("QT", [128, 2, 4, S], BF16)
        KT = A.alloc("KT", [128, 4, S], BF16)
        s.op("pool", lambda e: e.memset(QT.ap.rearrange("p a i t -> p (a i t)"), 0.0), writes=[QT])
        Vx = A.alloc("Vx", [128, NT, 8, 65], BF16)
        s.op("pool", lambda e: e.memset(Vx.ap.rearrange("p t h d -> p (t h d)"), 1.0), writes=[Vx])
        m1 = A.mark()
        W = A.alloc("winA", [128, 8, 1536], BF16)
        wsrc = I["w_in"][l].rearrange("(c p) f -> p c f", p=128)
        Wq = [W.sub("winA%d" % i) for i in range(3)]
        for i in range(3):
            s.dma("pool", W[:, :, i * 512:(i + 1) * 512], wsrc[:, :, i * 512:(i + 1) * 512], writes=[Wq[i]])
        if after_w is not None:
            after_w()
        FM = [P[2], P[3], P[4], P[5]]
        TM = [P[6], P[7]]
        cnt = [0, 0]

        def emit(b, h, mid):
            for i in range(8):
                bank = FM[cnt[0] % 4]
                cnt[0] += 1
                for c in range(8):
                    s.op("pe", lambda e, c=c, i=i, bank=bank: e.matmul(bank.ap, lhsT=W[:, c, i * 128:(i + 1) * 128],
                                                                      rhs=h[:, c, :], start=(c == 0), stop=(c == 7)),
                         reads=[h, Wq[i // 4]], writes=[bank], inc=(c == 7))
                if i < 4:
                    for a in range(2):
                        pl = slice(64 * a, 64 * a + 64)
                        s.op("act", lambda e, i=i, bank=bank, a=a, pl=pl: e.mul(
                            out=QT[pl, a, i, b * 512:(b + 1) * 512], in_=bank[pl, :], mul=0.125),
                             reads=[bank], writes=[QT.sub("e")])
                else:
                    s.op("dve", lambda e, i=i, bank=bank: e.tensor_copy(out=KT[:, i - 4, b * 512:(b + 1) * 512], in_=bank.ap),
                         reads=[bank], writes=[KT.sub("e")])
            mid()
            for j in range(4):
                t = b * 4 + j
                bank = TM[cnt[1] % 2]
                cnt[1] += 1
                for c in range(8):
                    s.op("pe", lambda e, c=c, j=j, bank=bank: e.matmul(bank.ap, lhsT=h[:, c, j * 128:(j + 1) * 128],
                                                                      rhs=W[:, c, 1024:1536], start=(c == 0), stop=(c == 7)),
                         reads=[h, Wq[2]], writes=[bank], inc=(c == 7))
                s.op("dve", lambda e, t=t, bank=bank: e.tensor_copy(
                    out=Vx[:, t, :, 0:64], in_=bank.ap.rearrange("p (h d) -> p h d", h=8)),
                     reads=[bank], writes=[Vx.sub("e")])

        self.proj_blocks(l, emit)
        for r_ in (QT, KT, Vx):
            r_.merge_subs()
        A.release(m1)
        self.dump("QT", QT)
        self.dump("winA", W)
        self.dump("KT", KT)
        self.dump("Vx", Vx)
        return QT, KT, Vx

    def na(self, l, QT, KT, Vx):
        A, s, P, I = self.A, self.s, self.P, self.I
        types, groups = self.types, self.groups
        NTY = len(types)
        maskS = A.alloc("na_mask", [128, NTY, 256], BF16)
        s.dma("pool", maskS.ap, I["na_mask"].rearrange("t p q -> p t q"), writes=[maskS])
        TbS = [A.alloc("tbs%d" % i, [128, 31, 64], F32) for i in range(2)]
        biasT = [A.alloc("biasT%d" % i, [128, NTY, 256], BF16) for i in range(2)]
        btf = A.alloc("btf", [128, NTY, 256], F32)
        PT = [A.alloc("pt%d" % i, [128, 512], BF16) for i in range(3)]
        PT2 = [A.alloc("ptm%d" % i, [128, 512], BF16) for i in range(3)]
        ost = [A.alloc("ost%d" % i, [128, 2, 64], F32) for i in range(2)]
        rec = A.alloc("narec", [128, 4], F32)
        for tb in TbS:
            s.op("pool", lambda e, tb=tb: e.memset(tb.ap.rearrange("p e c -> p (e c)"), 0.0), writes=[tb])
        Ob = [[P[2], P[3]], [P[4], P[5]]]

        def build(h):
            tb, bt = TbS[h % 2], biasT[h % 2]
            s.dma("sp", tb[0:64, :, :], I["na_tr"][l, h], writes=[tb])
            s.dma("sp", tb[64:128, 1:31, :], I["na_tr"][l, h, :, 0:30, :], writes=[tb])
            tb2 = tb.ap.rearrange("p e c -> p (e c)")
            for ty, (m, e0) in enumerate(types):
                off = (e0 + 8) * 64
                s.op("pool", lambda e, ty=ty, off=off: e.tensor_tensor(out=btf[:, ty, :], in0=tb2[:, off:off + 256],
                                                                      in1=maskS[:, ty, :], op=ALU.add),
                     reads=[tb, maskS], writes=[btf])

        def build_b(h):
            bt = biasT[h % 2]
            s.op("act", lambda e: e.activation(out=bt.ap, in_=btf.ap, func=AF.Exp), reads=[btf], writes=[bt])

        STb = [P[0], P[1], P[6], P[7]]
        NPT = 5
        PT = PT + [A.alloc("pt3", [128, 512], BF16), A.alloc("pt4", [128, 512], BF16)]
        PT2 = PT2 + [A.alloc("ptm3", [128, 512], BF16), A.alloc("ptm4", [128, 512], BF16)]
        PT2h = [[p_.sub("h0"), p_.sub("h1")] for p_ in PT2]
        osth = [[o_.sub("a0"), o_.sub("a1")] for o_ in ost]
        rech = [rec.sub("a0"), rec.sub("a1")]
        build(0)
        build_b(0)
        self.dump("biasT", biasT[0])
        self.dump("tbs", TbS[0])
        items = []
        ngrp = 0
        for h in range(NA_H):
            for g, chunks in enumerate(groups):
                lastidx = [max(j for j, c in enumerate(chunks) if c["rel"][a2]) for a2 in range(2)]
                first = [True, True]
                npairs = (len(chunks) + 1) // 2
                for pi in range(npairs):
                    pair = chunks[2 * pi:2 * pi + 2]
                    pv = []
                    for u, ch in enumerate(pair):
                        jj = 2 * pi + u
                        for a2 in range(2):
                            if ch["rel"][a2]:
                                pv.append((u, ch, a2, first[a2], jj == lastidx[a2]))
                                first[a2] = False
                    items.append(dict(h=h, g=g, pair=pair, pv=pv, gi=ngrp, newhead=(g == 0 and pi == 0),
                                      lastpair=(pi == npairs - 1), midhead=(g == len(groups) // 2 and pi == 0)))
                ngrp += 1

        def front(n):
            it = items[n]
            h, g, pair = it["h"], it["g"], it["pair"]
            if it["newhead"] and h + 1 < NA_H:
                build(h + 1)
            if it["midhead"] and h + 1 < NA_H:
                build_b(h + 1)
            bt = biasT[h % 2]
            i, a = h // 2, h % 2
            pl = slice(64 * a, 64 * a + 64)
            q0 = g * 256
            ST = STb[n % 4]
            pt0, pt = PT[n % NPT], PT2[n % NPT]
            pth = PT2h[n % NPT]
            for u, ch in enumerate(pair):
                kt = ch["kt"]
                s.op("pe", lambda e, u=u, kt=kt: e.matmul(
                    ST[:, u * 256:(u + 1) * 256], lhsT=KT[:, i, kt * 128:(kt + 1) * 128],
                    rhs=QT[:, a, i, q0:q0 + 256], start=True, stop=True),
                     reads=[KT, QT], writes=[ST], inc=(u == len(pair) - 1))
            w = 256 * len(pair)
            s.op("act", lambda e: e.activation(out=pt0[:, 0:w], in_=ST[:, 0:w], func=AF.Exp), reads=[ST], writes=[pt0])
            for u, ch in enumerate(pair):
                s.op("dve", lambda e, u=u, ch=ch: e.tensor_tensor(
                    out=pt[:, u * 256:(u + 1) * 256], in0=pt0[:, u * 256:(u + 1) * 256], in1=bt[:, ch["ty"], :],
                    op=ALU.mult), reads=[pt0, bt], writes=[pth[u]])

        def back(n):
            it = items[n]
            h, g = it["h"], it["g"]
            q0 = g * 256
            O = Ob[it["gi"] % 2]
            os_ = ost[it["gi"] % 2]
            pt = PT2[n % NPT]
            pth = PT2h[n % NPT]
            osh = osth[it["gi"] % 2]
            for (u, ch, a2, st, sp_) in it["pv"]:
                s.op("pe", lambda e, u=u, ch=ch, a2=a2, st=st, sp_=sp_: e.matmul(
                    O[a2][:, 0:65], lhsT=pt[:, u * 256 + a2 * 128:u * 256 + a2 * 128 + 128],
                    rhs=Vx[:, ch["kt"], h, :], start=st, stop=sp_),
                     reads=[pth[u], Vx], writes=[O[a2]], inc=sp_)
            if it["lastpair"]:
                for a2 in range(2):
                    s.op("dve", lambda e, a2=a2: e.reciprocal(out=rec[:, a2:a2 + 1], in_=O[a2][:, 64:65]),
                         reads=[O[a2]], writes=[rech[a2]])
                    s.op("dve", lambda e, a2=a2: e.tensor_scalar_mul(out=os_[:, a2, :], in0=O[a2][:, 0:64],
                                                                    scalar1=rec[:, a2:a2 + 1]),
                         reads=[O[a2], rech[a2]], writes=[osh[a2]])
                dst = self.nao.ap[q0:q0 + 256, h * 64:(h + 1) * 64].rearrange("(a p) d -> p a d", p=128)
                s.dma("sp", dst, os_.ap, reads=[osh[0], osh[1]], writes=[self.nao.t[2 * g], self.nao.t[2 * g + 1]])

        LOOK = 3
        for n in range(len(items) + LOOK):
            if n < len(items):
                front(n)
            if n >= LOOK:
                back(n - LOOK)

    def proj_gla(self, l, W, Wq):
        A, s, P, I = self.A, self.s, self.P, self.I
        S, NT = self.S, self.NT
        R = {}
        R["gqT"] = A.alloc("gqT", [128, 2, S], BF16)
        R["gkT"] = A.alloc("gkT", [128, 2, S], BF16)
        R["gk"] = A.alloc("gk", [128, NT, 256], BF16)
        R["gv"] = A.alloc("gv", [128, NT, 512], BF16)
        m1 = A.mark()
        wgt = A.alloc("wgt", [128, 2, 256], F32)
        for d in range(2):
            s.dma("sp", wgt[0:17, d, :], I["wgate"][l, d], writes=[wgt])
        codeT = [[A.alloc("codeT%d%d" % (d, i), [128, 512], F32) for i in range(2)] for d in range(2)]
        for d in range(2):
            for i in range(2):
                s.op("pool", lambda e, d=d, i=i: e.memset(codeT[d][i][0:32, :], 1.0), writes=[codeT[d][i]])
        e1 = [A.alloc("ge1%d" % i, [128, 512], F32) for i in range(2)]
        grst = [A.alloc("grst%d" % i, [128, 512], BF16) for i in range(2)]
        sil = [A.alloc("sil%d" % i, [128, 512], F32) for i in range(4)]
        FM = [P[2], P[3]]
        TM = [P[4], P[5]]
        GBs = [P[6], P[7]]
        cnt = [0, 0]
        rsb = [A.alloc("rsb%d" % i, [128, 512], F32) for i in range(2)]
        gqT, gkT, gk, gv = R["gqT"], R["gkT"], R["gk"], R["gv"]
        spst = [A.alloc("spst%d" % i, [128, 512], F32) for i in range(2)]

        def emit(b, h, mid):
            for i in range(4):
                bank = FM[cnt[0] % 2]
                cnt[0] += 1
                for c in range(8):
                    s.op("pe", lambda e, c=c, i=i, bank=bank: e.matmul(bank.ap, lhsT=W[:, c, i * 128:(i + 1) * 128],
                                                                      rhs=h[:, c, :], start=(c == 0), stop=(c == 7)),
                         reads=[h, Wq[0]], writes=[bank], inc=(c == 7))
                if i < 2:
                    s.op("act", lambda e, i=i, bank=bank: e.mul(out=gqT[:, i, b * 512:(b + 1) * 512], in_=bank.ap, mul=0.125),
                         reads=[bank], writes=[gqT.sub("e")])
                else:
                    s.op("dve", lambda e, i=i, bank=bank: e.tensor_copy(out=gkT[:, i - 2, b * 512:(b + 1) * 512], in_=bank.ap),
                         reads=[bank], writes=[gkT.sub("e")])
            for d in range(2):
                bank = FM[cnt[0] % 2]
                cnt[0] += 1
                ct = codeT[d][b % 2]
                for c in range(8):
                    s.op("pe", lambda e, c=c, d=d, bank=bank: e.matmul(bank[0:16, :], lhsT=W[:, c, 1536 + 16 * d:1552 + 16 * d],
                                                                      rhs=h[:, c, :], start=(c == 0), stop=(c == 7)),
                         reads=[h, Wq[3]], writes=[bank], inc=(c == 7))
                s.op("dve", lambda e, bank=bank, ct=ct: e.tensor_copy(out=ct[0:16, :], in_=bank[0:16, :]),
                     reads=[bank], writes=[ct])
            mid()
            for j in range(4):
                t = b * 4 + j
                GB = GBs[t % 2]
                for d in range(2):
                    ct = codeT[d][b % 2]
                    s.op("pe", lambda e, d=d, j=j, ct=ct: e.matmul(GB[:, d * 256:(d + 1) * 256],
                                                                  lhsT=ct[0:17, j * 128:(j + 1) * 128], rhs=wgt[0:17, d, :],
                                                                  start=True, stop=True),
                         reads=[ct, wgt], writes=[GB], inc=(d == 1))
                ee = e1[t % 2]
                s.op("act", lambda e, ee=ee: e.activation(out=ee.ap, in_=GB.ap, func=AF.Exp, scale=-1.0),
                     reads=[GB], writes=[ee])
                st = spst[t % 2]
                s.op("act", lambda e, ee=ee, st=st: e.activation(out=st.ap, in_=ee.ap, func=AF.Ln, bias=1.0),
                     reads=[ee], writes=[st])
                s.dma("sp", self.spd.rows(t), st.ap, reads=[st], writes=[self.spd.t[t]])
                for (lo, hi, kind) in ((256, 512, "k"), (512, 1024, "v"), (1024, 1536, "r")):
                    bank = TM[cnt[1] % 2]
                    cnt[1] += 1
                    n = hi - lo
                    for c in range(8):
                        s.op("pe", lambda e, c=c, j=j, bank=bank, lo=lo, hi=hi, n=n: e.matmul(
                            bank[:, 0:n], lhsT=h[:, c, j * 128:(j + 1) * 128], rhs=W[:, c, lo:hi],
                            start=(c == 0), stop=(c == 7)),
                             reads=[h, Wq[lo // 512]], writes=[bank], inc=(c == 7))
                    if kind == "k":
                        s.op("dve", lambda e, t=t, bank=bank: e.tensor_copy(out=gk[:, t, :], in_=bank[:, 0:256]),
                             reads=[bank], writes=[gk.sub("e")])
                    elif kind == "v":
                        s.op("dve", lambda e, t=t, bank=bank: e.tensor_copy(out=gv[:, t, :], in_=bank.ap),
                             reads=[bank], writes=[gv.sub("e")])
                    else:
                        gs = grst[t % 2]
                        sa, sb = sil[2 * (t % 2)], sil[2 * (t % 2) + 1]
                        rs = rsb[t % 2]
                        s.op("dve", lambda e, rs=rs, bank=bank: e.tensor_copy(out=rs.ap, in_=bank.ap), reads=[bank], writes=[rs])
                        s.op("act", lambda e, rs=rs: e.activation(out=sa.ap, in_=rs.ap, func=AF.Exp, scale=-1.0),
                             reads=[rs], writes=[sa])
                        s.op("act", lambda e: e.activation(out=sb.ap, in_=sa.ap, func=AF.Ln, bias=1.0),
                             reads=[sa], writes=[sb])
                        s.op("act", lambda e: e.activation(out=sa.ap, in_=sb.ap, func=AF.Exp, scale=-1.0),
                             reads=[sb], writes=[sa])
                        s.op("pool", lambda e, gs=gs, rs=rs: e.tensor_tensor(out=gs.ap, in0=rs.ap, in1=sa.ap, op=ALU.mult),
                             reads=[rs, sa], writes=[gs])
                        s.dma("sp", self.grs.rows(t), gs.ap, reads=[gs], writes=[self.grs.t[t]])

        self.proj_blocks(l, emit)
        for r_ in (gqT, gkT, gk, gv):
            r_.merge_subs()
        A.release(m1)
        return R

    def gla(self, l, R):
        A, s, P, I = self.A, self.s, self.P, self.I
        NT = self.NT
        H2 = NT // 2
        gqT, gkT, gk, gv = R["gqT"], R["gkT"], R["gk"], R["gv"]
        cm = {}
        for n in ("gla_ltf", "gla_uf", "gla_ltb", "gla_ub"):
            cm[n] = A.alloc(n, [128, 128], F32)
            s.dma("sp", cm[n].ap, I[n], writes=[cm[n]])
        for n in ("gla_mf", "gla_mb"):
            cm[n] = A.alloc(n, [128, 512], BF16)
            s.dma("pool", cm[n].ap, I[n], writes=[cm[n]])
        ggain = A.alloc("ggain", [128, 128], F32)
        s.dma("sp", ggain.ap, I["gla_gain"][l:l + 1, :].partition_broadcast(128), writes=[ggain])
        NF = 6
        osum3 = [A.alloc("gos%d" % i, [128, 512], F32) for i in range(NF)]
        grt3 = [A.alloc("ggr%d" % i, [128, 512], BF16) for i in range(NF)]
        ss3 = [A.alloc("gss%d" % i, [128, 32], F32) for i in range(NF)]
        tmp = [A.alloc("gtmp%d" % i, [128, 512], F32) for i in range(4)]
        got = [A.alloc("ggo%d" % i, [128, 512], BF16) for i in range(4)]
        junk4 = [A.alloc("gjunk%d" % i, [128, 128], BF16) for i in range(4)]
        ssq3 = [[x_.sub("c%d" % i) for i in range(4)] for x_ in ss3]
        tmq = [[x_.sub("h%d" % i) for i in range(4)] for x_ in tmp]
        Abk = [P[1], P[6]]
        v3 = lambda ap: ap.rearrange("p (a b) -> p a b", a=2)
        dirs = []
        for di, dirn in enumerate(("f", "b")):
            d = dict(dirn=dirn)
            d["LT"] = cm["gla_ltf"] if dirn == "f" else cm["gla_ltb"]
            d["UU"] = cm["gla_uf"] if dirn == "f" else cm["gla_ub"]
            d["MK"] = cm["gla_mf"] if dirn == "f" else cm["gla_mb"]
            d["so"] = 0 if dirn == "f" else 256
            d["lastcol"] = 127 if dirn == "f" else 0
            d["order"] = list(range(NT)) if dirn == "f" else list(range(NT - 1, -1, -1))
            d["S32"] = A.alloc("S32" + dirn, [128, 256], F32)
            d["S32q"] = [[d["S32"].sub("q%d%d" % (pr, a)) for a in range(2)] for pr in range(2)]
            d["Sb"] = A.alloc("Sb" + dirn, [128, 256], BF16)
            for nm, shp, dt in (("E", [128, 512], F32), ("ENB", [128, 256], F32), ("qe", [128, 256], BF16),
                                ("ke", [128, 256], BF16), ("kl", [128, 256], BF16), ("AM", [128, 512], BF16),
                                ("of", [128, 512], F32)):
                d[nm] = [A.alloc("g%s%s%d" % (nm, dirn, i), shp, dt) for i in range(2)]
            d["sp"] = [A.alloc("sptl%s%d" % (dirn, i), [128, 256], F32) for i in range(4)]
            d["O"] = P[2 + di]
            d["D"] = P[4 + di]
            d["BL"] = [P[0], P[7]][di]
            s.op("dve", lambda e, d=d: e.memset(d["S32"].ap, 0.0), writes=[q for r in d["S32q"] for q in r])
            d["AMh"] = [[am.sub("a0"), am.sub("a1")] for am in d["AM"]]
            s.op("dve", lambda e, d=d: e.memset(d["Sb"].ap, 0.0), writes=[d["Sb"]])
            dirs.append(d)

        def load_sp(d, idx):
            t = d["order"][idx]
            sp = d["sp"][idx % 4]
            s.dma("sp", sp.ap, self.spd.rows(t)[:, d["so"]:d["so"] + 256], reads=[self.spd.t[t]], writes=[sp])

        def stageA(d, idx, part):
            t = d["order"][idx]
            k = idx % 2
            sp = d["sp"][idx % 4]
            spt = sp.ap
            BL = d["BL"]
            E, ENB, qe, ke, kl, AM = d["E"][k], d["ENB"][k], d["qe"][k], d["ke"][k], d["kl"][k], d["AM"][k]
            LT, UU, MK = d["LT"], d["UU"], d["MK"]
            if part == 2:
                for hh in range(4):
                    pr, a = hh // 2, hh % 2
                    pl = slice(64 * a, 64 * a + 64)
                    s.op("pe", lambda e, pr=pr, pl=pl, a=a: e.matmul(
                        Abk[a][:, pr * 128:(pr + 1) * 128], lhsT=ke[pl, pr * 128:(pr + 1) * 128],
                        rhs=qe[pl, pr * 128:(pr + 1) * 128], start=True, stop=True),
                         reads=[ke, qe], writes=[Abk[a]], inc=(hh >= 2))
                am4 = AM.ap.rearrange("p (pr a t) -> p pr a t", pr=2, a=2)
                for a in range(2):
                    s.op("dve", lambda e, a=a: e.tensor_tensor(out=am4[:, :, a, :], in0=v3(Abk[a][:, 0:256]),
                                                              in1=v3(MK[:, 0:256]), op=ALU.mult),
                         reads=[Abk[a], MK], writes=[d["AMh"][k][a]])
                if idx > H2:
                    s.dma("sp", d["of"][k].ap, self.ofw.rows(t), reads=[self.ofw.t[t]], writes=[d["of"][k]])
                return
            for pr in range(2):
                s.op("pe", lambda e, pr=pr: e.matmul(BL[:, pr * 128:(pr + 1) * 128], lhsT=spt[:, pr * 128:(pr + 1) * 128],
                                                    rhs=LT.ap, start=True, stop=True),
                     reads=[sp, LT], writes=[BL], inc=False)
            s.op("pe", lambda e: e.matmul(BL[:, 256:512], lhsT=UU.ap, rhs=spt, start=True, stop=True),
                 reads=[sp, UU], writes=[BL])
            s.op("act", lambda e: e.activation(out=E.ap, in_=BL.ap, func=AF.Exp), reads=[BL], writes=[E])
            s.op("act", lambda e: e.activation(out=ENB.ap, in_=BL[:, 0:256], func=AF.Exp, scale=-1.0),
                 reads=[BL], writes=[ENB])
            s.op("dve", lambda e: e.tensor_tensor(out=v3(qe.ap), in0=gqT[:, :, t * 128:(t + 1) * 128],
                                                  in1=v3(E[:, 0:256]), op=ALU.mult),
                 reads=[gqT, E], writes=[qe])
            s.op("pool", lambda e: e.tensor_tensor(out=v3(ke.ap), in0=gkT[:, :, t * 128:(t + 1) * 128],
                                                   in1=v3(ENB.ap), op=ALU.mult),
                 reads=[gkT, ENB], writes=[ke])
            s.op("pool", lambda e: e.tensor_tensor(out=kl.ap, in0=gk[:, t, :], in1=E[:, 256:512], op=ALU.mult),
                 reads=[gk, E], writes=[kl])

        fin_items = []

        def stageB(d, idx):
            t = d["order"][idx]
            k = idx % 2
            E, qe, kl, AM, of = d["E"][k], d["qe"][k], d["kl"][k], d["AM"][k], d["of"][k]
            S32, Sb, D_, O_ = d["S32"], d["Sb"], d["D"], d["O"]
            if idx == H2:
                s.dma("sp", of.ap, self.ofw.rows(t), reads=[self.ofw.t[t]], writes=[of])
            for pr in range(2):
                s.op("pe", lambda e, pr=pr: e.matmul(D_[:, pr * 256:(pr + 1) * 256], lhsT=kl[:, pr * 128:(pr + 1) * 128],
                                                    rhs=gv[:, t, pr * 256:(pr + 1) * 256], start=True, stop=True),
                     reads=[kl, gv], writes=[D_], inc=(pr == 1))
            for hh in range(4):
                pr, a = hh // 2, hh % 2
                pl = slice(64 * a, 64 * a + 64)
                s.op("pe", lambda e, hh=hh: e.matmul(O_[:, hh * 128:(hh + 1) * 128], lhsT=AM[:, hh * 128:(hh + 1) * 128],
                                                    rhs=gv[:, t, hh * 128:(hh + 1) * 128], start=True, stop=False),
                     reads=[d["AMh"][k][hh % 2], gv], writes=[O_], inc=False)
                s.op("pe", lambda e, hh=hh, pr=pr, pl=pl: e.matmul(
                    O_[:, hh * 128:(hh + 1) * 128], lhsT=qe[pl, pr * 128:(pr + 1) * 128],
                    rhs=Sb[pl, pr * 128:(pr + 1) * 128], start=False, stop=True),
                     reads=[qe, Sb], writes=[O_], inc=(hh == 3))
            for pr in range(2):
                for a in range(2):
                    pl = slice(64 * a, 64 * a + 64)
                    col = pr * 128 + d["lastcol"]
                    s.op("dve", lambda e, pr=pr, a=a, pl=pl, col=col: e.scalar_tensor_tensor(
                        out=S32[pl, pr * 128:(pr + 1) * 128], in0=S32[pl, pr * 128:(pr + 1) * 128],
                        scalar=E[pl, col:col + 1], in1=D_[pl, pr * 256 + a * 128:pr * 256 + a * 128 + 128],
                        op0=ALU.mult, op1=ALU.add), reads=[d["S32q"][pr][a], E, D_], writes=[d["S32q"][pr][a]])
            s.op("act", lambda e: e.copy(out=Sb.ap, in_=S32.ap), reads=[q for r in d["S32q"] for q in r], writes=[Sb])
            if idx < H2:
                s.op("act", lambda e: e.copy(out=of.ap, in_=O_.ap), reads=[O_], writes=[of])
                s.dma("sp", self.ofw.rows(t), of.ap, reads=[of], writes=[self.ofw.t[t]])
            else:
                fi = len(fin_items)
                fin_items.append(t)
                os_ = osum3[fi % NF]
                s.op("dve", lambda e: e.tensor_tensor(out=os_.ap, in0=O_.ap, in1=of.ap, op=ALU.add),
                     reads=[O_, of], writes=[os_])

        def fin1(fi):
            t = fin_items[fi]
            os_, ss, gr = osum3[fi % NF], ss3[fi % NF], grt3[fi % NF]
            s.dma("sp", gr.ap, self.grs.rows(t), reads=[self.grs.t[t]], writes=[gr])
            ssq = ssq3[fi % NF]
            s.op("dve", lambda e: e.memset(ss[:, 0:4], 0.0), writes=[ss] + ssq)
            for hh in range(4):
                s.op("act", lambda e, hh=hh: e.activation(out=junk4[hh].ap, in_=os_[:, hh * 128:(hh + 1) * 128],
                                                         func=AF.Square, accum_out=ss[:, hh:hh + 1]),
                     reads=[os_], writes=[junk4[hh], ssq[hh]])
            s.op("dve", lambda e: e.tensor_scalar(out=ss[:, 8:12], in0=ss[:, 0:4], scalar1=1.0 / 128, scalar2=EPS,
                                                  op0=ALU.mult, op1=ALU.add), reads=[ss] + ssq, writes=[ss])
            s.op("pool", lambda e: e.tensor_tensor(out=ss[:, 16:20], in0=ss[:, 8:12], in1=self.mhalf[:, 0:4],
                                                   op=ALU.pow), reads=[ss, self.mhalf], writes=[ss])

        def fin2(fi):
            t = fin_items[fi]
            os_, ss, gr = osum3[fi % NF], ss3[fi % NF], grt3[fi % NF]
            tm, go = tmp[fi % 4], got[fi % 4]
            for hh in range(4):
                s.op("dve", lambda e, hh=hh: e.scalar_tensor_tensor(
                    out=tm[:, hh * 128:(hh + 1) * 128], in0=os_[:, hh * 128:(hh + 1) * 128],
                    scalar=ss[:, 16 + hh:17 + hh], in1=ggain.ap, op0=ALU.mult, op1=ALU.mult),
                     reads=[os_, ss, ggain], writes=[tmq[fi % 4][hh]])
            s.op("pool", lambda e: e.tensor_tensor(out=go.ap, in0=tm.ap, in1=gr.ap, op=ALU.mult),
                 reads=tmq[fi % 4] + [gr], writes=[go])
            s.dma("sp", self.glo.rows(t), go.ap, reads=[go], writes=[self.glo.t[t]])

        for d in dirs:
            load_sp(d, 0)
            if NT > 1:
                load_sp(d, 1)
        for part in (1, 2):
            for d in dirs:
                stageA(d, 0, part)
        n1 = n2 = 0
        for idx in range(NT):
            for d in dirs:
                if idx + 2 < NT:
                    load_sp(d, idx + 2)
            if idx + 1 < NT:
                for part in (1, 2):
                    for d in dirs:
                        stageA(d, idx + 1, part)
            nf_before = len(fin_items)
            for d in dirs:
                stageB(d, idx)
            while n2 < n1:
                fin2(n2)
                n2 += 1
            while n1 < nf_before:
                fin1(n1)
                n1 += 1
        while n1 < len(fin_items):
            fin1(n1)
            n1 += 1
        while n2 < len(fin_items):
            fin2(n2)
            n2 += 1
        self.tile_order = list(fin_items)

    def wout(self, l, pre=None):
        A, s, P, I = self.A, self.s, self.P, self.I
        m0 = A.mark()
        W = A.alloc("wout", [128, 8, D], BF16)
        s.dma("pool", W.ap, I["w_out"][l].rearrange("(c p) d -> p c d", p=128), writes=[W])
        if pre is not None:
            self.ffn_load_gu(pre)
        gain = self.load_gain("na_gain", l, 512)
        nat = [A.alloc("wnat%d" % i, [128, 512], F32) for i in range(2)]
        mm = [A.alloc("wm%d" % i, [128, D], BF16) for i in range(2)]
        xt = [A.alloc("wx%d" % i, [128, D], F32) for i in range(3)]
        mT = [A.alloc("wmT%d" % i, [128, 8, 128], BF16) for i in range(2)]
        junk = A.alloc("wjunk", [128, 512], BF16)
        ss = A.alloc("wss", [128, 32], F32)
        Tb = [P[0], P[1]]
        Yb = [P[2], P[3], P[4], P[5]]
        seqpos = {}

        def front(t):
            q = seqpos.setdefault(t, len(seqpos))
            na_, m, x = nat[q % 2], mm[q % 2], xt[q % 3]
            s.dma("sp", na_.ap, self.nao.rows(t), reads=[self.nao.t[t]], writes=[na_])
            s.dma("sp", m[:, 512:1024], self.glo.rows(t), reads=[self.glo.t[t]], writes=[m])
            s.dma("sp", x.ap, self.xs.rows(t), reads=[self.xs.t[t]], writes=[x])
            s.op("dve", lambda e: e.memset(ss[:, 0:1], 0.0), writes=[ss])
            s.op("act", lambda e: e.activation(out=junk.ap, in_=na_.ap, func=AF.Square, accum_out=ss[:, 0:1]),
                 reads=[na_], writes=[junk, ss])
            s.op("dve", lambda e: e.tensor_scalar(out=ss[:, 8:9], in0=ss[:, 0:1], scalar1=1.0 / 512, scalar2=EPS,
                                                  op0=ALU.mult, op1=ALU.add), reads=[ss], writes=[ss])
            s.op("pool", lambda e: e.tensor_tensor(out=ss[:, 16:17], in0=ss[:, 8:9], in1=self.mhalf[:, 0:1], op=ALU.pow),
                 reads=[ss, self.mhalf], writes=[ss])
            s.op("dve", lambda e: e.scalar_tensor_tensor(out=m[:, 0:512], in0=na_.ap, scalar=ss[:, 16:17],
                                                         in1=gain.ap, op0=ALU.mult, op1=ALU.mult),
                 reads=[na_, ss, gain], writes=[m])

        def back(t):
            q = seqpos[t]
            m, x, mt = mm[q % 2], xt[q % 3], mT[q % 2]
            self.transpose_to(m, Tb[q % 2], mt, mt.ap)
            for hf in range(2):
                Y = Yb[(2 * q + hf) % 4]
                for c in range(8):
                    s.op("pe", lambda e, c=c, hf=hf, Y=Y: e.matmul(Y.ap, lhsT=mt[:, c, :],
                                                                  rhs=W[:, c, hf * 512:(hf + 1) * 512],
                                                                  start=(c == 0), stop=(c == 7)),
                         reads=[mt, W], writes=[Y], inc=(c == 7))
                s.op("dve", lambda e, hf=hf, Y=Y: e.tensor_tensor(out=x[:, hf * 512:(hf + 1) * 512], in0=Y.ap,
                                                                 in1=x[:, hf * 512:(hf + 1) * 512], op=ALU.add),
                     reads=[Y, x], writes=[x])
            s.dma("pool", self.xs.rows(t), x.ap, reads=[x], writes=[self.xs.t[t]])

        order = getattr(self, "tile_order", None) or list(range(self.NT))
        assert sorted(order) == list(range(self.NT))
        front(order[0])
        for i, t in enumerate(order):
            if i + 1 < self.NT:
                front(order[i + 1])
            back(t)
        pos = {t: i for i, t in enumerate(order)}
        self.block_order = sorted(range(self.NB), key=lambda b: max(pos[4 * b + j] for j in range(4)))
        A.release(m0)


_CACHE = {}


def _prog(S, L):
    key = (S, L)
    if key not in _CACHE:
        p = Prog(S, L)
        p.build()
        _CACHE[key] = p
    return _CACHE[key]


PARAMS = ("ffn1_norm", "ffn1_wg", "ffn1_wu", "ffn1_wd", "mix_norm", "w_in", "na_gain", "gla_gain", "w_out",
          "ffn2_norm", "ffn2_wg", "ffn2_wu", "ffn2_wd")


def make_in_maps(inputs, p, ncores):
    L = p.L
    shared = {}
    for n in PARAMS:
        shared[n] = np.ascontiguousarray(np.asarray(inputs[n], np.float32))
    shared["final_norm"] = np.ascontiguousarray(np.asarray(inputs["final_norm"], np.float32).reshape(1, D))
    shared.update(host_layout(inputs, L))
    shared.update(p.consts)
    x = np.asarray(inputs["x"], np.float32)
    maps = []
    for c in range(ncores):
        m = dict(shared)
        m["x"] = np.ascontiguousarray(x[c])
        maps.append(m)
    return maps


def kernel(**inputs):
    x = np.asarray(inputs["x"])
    B, S, _ = x.shape
    L = np.asarray(inputs["ffn1_norm"]).shape[0]
    p = _prog(S, L)
    maps = make_in_maps(inputs, p, B)
    res = run_bass_kernel_spmd(p.nc, maps, core_ids=list(range(B)))
    return np.stack([np.asarray(r["out"], np.float32) for r in res.results], axis=0)
```
